# Optimizing a Trainium2 kernel written in Bass

```python
import jax
import jax.numpy as jnp
from jax import lax
import numpy as np

D_MODEL = 1024
BATCH = 16
SEQ = 256
DEPTH = 4
DEC_BATCH = 8
DEC_SEQ = 4096
PAST_LEN = 256

GRID_W = 64
N_EVEN = (DEPTH + 1) // 2
N_ODD = DEPTH // 2
EPS = 1e-6

GDN_HEADS = 4
GDN_DK = 128
GDN_DV = 128
GDN_CONV = 3
GDN_CHUNK = 64
A_QK = GDN_HEADS * GDN_DK
A_V = GDN_HEADS * GDN_DV
A_CONV = 2 * A_QK + A_V
FNET_GROUPS = 4
FNET_GROUP_DIM = 128
B_W = FNET_GROUPS * FNET_GROUP_DIM
EVEN_IN = A_CONV + A_V + 4 * GDN_HEADS + B_W
EVEN_MIX = A_V + B_W
MLA_HEADS = 8
MLA_Q_LORA = 512
MLA_KV_LORA = 256
MLA_NOPE = 128
MLA_ROPE = 64
MLA_V = 128
MLA_QK = MLA_NOPE + MLA_ROPE
ODD_IN = MLA_Q_LORA + MLA_KV_LORA + MLA_ROPE
ROPE_BASE = 10000.0
Q_BLOCK = 128
PEER_HEADS = 8
PEER_NKEYS = 128
PEER_EXPERTS = PEER_NKEYS * PEER_NKEYS
PEER_DQ = 256
PEER_HALF = PEER_DQ // 2
PEER_TOPK = 16
PEER_BLOCK = 128

kernel_name = 'gdn_fnet_mla_peer_flow_step'


def rmsnorm(x, w):
    xf = x.astype(jnp.float32)
    y = xf * lax.rsqrt(jnp.mean(xf * xf, -1, keepdims=True) + EPS)
    return (y * w.astype(jnp.float32)).astype(x.dtype)


def l2norm(x):
    xf = x.astype(jnp.float32)
    return (xf * lax.rsqrt(jnp.sum(xf * xf, -1, keepdims=True) + EPS)).astype(x.dtype)


def ada_mods(cvec, w_mod, b_mod):
    m = (jax.nn.silu(cvec) @ w_mod + b_mod)[:, None, :]
    return jnp.split(m, 6, axis=-1)


def short_conv(x, w):
    pad = (GDN_CONV - 1) // 2
    L = x.shape[1]
    xp = jnp.pad(x, ((0, 0), (pad, pad), (0, 0)))
    y = xp[:, 0:L] * w[0]
    for i in range(1, GDN_CONV):
        y = y + xp[:, i:i + L] * w[i]
    return jax.nn.silu(y)


def gdn_chunked(q, k, v, g, beta, s0):
    f32 = jnp.float32
    B, L, H, DK = q.shape
    DV = v.shape[-1]
    C = GDN_CHUNK
    N = L // C
    qc = q.astype(f32).reshape(B, N, C, H, DK).transpose(1, 0, 3, 2, 4) * (DK ** -0.5)
    kc = k.astype(f32).reshape(B, N, C, H, DK).transpose(1, 0, 3, 2, 4)
    vc = v.astype(f32).reshape(B, N, C, H, DV).transpose(1, 0, 3, 2, 4)
    gc = jnp.cumsum(g.astype(f32).reshape(B, N, C, H).transpose(1, 0, 3, 2), axis=-1)
    bc = beta.astype(f32).reshape(B, N, C, H).transpose(1, 0, 3, 2)
    tri = jnp.tril(jnp.ones((C, C), bool))
    strict = jnp.tril(jnp.ones((C, C), bool), -1)
    decay = jnp.exp(jnp.where(tri, gc[..., :, None] - gc[..., None, :], -jnp.inf))
    kb = kc * bc[..., None]
    a = jnp.where(strict, jnp.einsum('nbhid,nbhjd->nbhij', kb, kc) * decay, 0.0)
    rhs = jnp.concatenate([vc * bc[..., None], kb * jnp.exp(gc)[..., None]], -1)
    sol = lax.linalg.triangular_solve(a, rhs, left_side=True, lower=True, unit_diagonal=True)
    u = sol[..., :DV]
    w = sol[..., DV:]
    qk = jnp.einsum('nbhid,nbhjd->nbhij', qc, kc) * decay

    def step(S, xs):
        q_i, k_i, u_i, w_i, g_i, qk_i = xs
        v_new = u_i - jnp.einsum('bhcd,bhde->bhce', w_i, S)
        o_i = (jnp.einsum('bhcd,bhde->bhce', q_i * jnp.exp(g_i)[..., None], S)
               + jnp.einsum('bhij,bhje->bhie', qk_i, v_new))
        g_last = g_i[..., -1:]
        S = (S * jnp.exp(g_last)[..., None]
             + jnp.einsum('bhcd,bhce->bhde', k_i * jnp.exp(g_last - g_i)[..., None], v_new))
        return S, o_i

    S, o = lax.scan(step, s0.astype(f32), (qc, kc, u, w, gc, qk))
    o = o.transpose(1, 0, 3, 2, 4).reshape(B, L, H, DV)
    return o.astype(q.dtype), S


def even_mixer(h, s_fwd, s_bwd, w_in, conv_w, a_log, dt_bias, o_norm_w, w_out):
    f32 = jnp.float32
    B, L, _ = h.shape
    p = h @ w_in
    qkv = short_conv(p[..., :A_CONV], conv_w)
    q = l2norm(qkv[..., :A_QK].reshape(B, L, GDN_HEADS, GDN_DK))
    k = l2norm(qkv[..., A_QK:2 * A_QK].reshape(B, L, GDN_HEADS, GDN_DK))
    v = qkv[..., 2 * A_QK:].reshape(B, L, GDN_HEADS, GDN_DV)
    gate = p[..., A_CONV:A_CONV + A_V].reshape(B, L, GDN_HEADS, GDN_DV)
    off = A_CONV + A_V
    ba = p[..., off:off + 4 * GDN_HEADS].astype(f32).reshape(B, L, 4, GDN_HEADS)
    beta = jax.nn.sigmoid(ba[:, :, :2])
    g = -jnp.exp(a_log.astype(f32)) * jax.nn.softplus(ba[:, :, 2:] + dt_bias.astype(f32))
    flip = lambda t: jnp.flip(t, 1)
    o_f, s_fwd = gdn_chunked(q, k, v, g[:, :, 0], beta[:, :, 0], s_fwd)
    o_b, s_bwd = gdn_chunked(flip(q), flip(k), flip(v), flip(g[:, :, 1]), flip(beta[:, :, 1]), s_bwd)
    o = o_f + flip(o_b)
    o = rmsnorm(o, o_norm_w) * jax.nn.silu(gate)
    xb = p[..., off + 4 * GDN_HEADS:].reshape(B, L, FNET_GROUPS, FNET_GROUP_DIM)
    fb = jnp.fft.fft2(xb.astype(f32), axes=(1, 3), norm='ortho').real.astype(h.dtype)
    mix = jnp.concatenate([o.reshape(B, L, A_V), fb.reshape(B, L, B_W)], -1)
    return mix @ w_out, s_fwd, s_bwd


def axial_angles(L):
    f32 = jnp.float32
    rows = L // GRID_W
    row = jnp.repeat(jnp.arange(rows, dtype=f32), GRID_W)
    col = jnp.tile(jnp.arange(GRID_W, dtype=f32), rows)
    n = MLA_ROPE // 4
    inv = jnp.power(ROPE_BASE, -jnp.arange(n, dtype=f32) / n)
    return row[:, None] * inv, col[:, None] * inv


def rotate(x, ang):
    n = x.shape[-1] // 2
    cos = jnp.cos(ang)[None, :, None, :]
    sin = jnp.sin(ang)[None, :, None, :]
    x1 = x[..., :n].astype(jnp.float32)
    x2 = x[..., n:].astype(jnp.float32)
    return jnp.concatenate([x1 * cos - x2 * sin, x1 * sin + x2 * cos], -1)


def rope_latent(x, ang_r, ang_c):
    half = MLA_ROPE // 2
    xr = x[..., MLA_NOPE:]
    rot = jnp.concatenate([rotate(xr[..., :half], ang_r), rotate(xr[..., half:], ang_c)], -1)
    return jnp.concatenate([x[..., :MLA_NOPE], rot.astype(x.dtype)], -1)


def mla_compress(h, w_in, q_norm_w, kv_norm_w):
    p = h @ w_in
    cq = rmsnorm(p[..., :MLA_Q_LORA], q_norm_w)
    ckv = rmsnorm(p[..., MLA_Q_LORA:MLA_Q_LORA + MLA_KV_LORA], kv_norm_w)
    kr = p[..., MLA_Q_LORA + MLA_KV_LORA:]
    return cq, ckv, kr


def mla_queries(cq, w_uq, qn_w):
    B, L, _ = cq.shape
    return rmsnorm((cq @ w_uq).reshape(B, L, MLA_HEADS, MLA_QK), qn_w)


def mla_keys_values(ckv, kr, w_ukv, kn_w):
    B, L, _ = ckv.shape
    kv = (ckv @ w_ukv).reshape(B, L, MLA_HEADS, MLA_NOPE + MLA_V)
    k_rope = jnp.broadcast_to(kr[:, :, None, :], (B, L, MLA_HEADS, MLA_ROPE))
    k = rmsnorm(jnp.concatenate([kv[..., :MLA_NOPE], k_rope], -1), kn_w)
    return k, kv[..., MLA_NOPE:]


def attend(q, k, v):
    B, Lq, H, dq = q.shape
    dv = v.shape[-1]
    nb = Lq // Q_BLOCK
    qb = q.reshape(B, nb, Q_BLOCK, H, dq).transpose(1, 0, 2, 3, 4)
    scale = dq ** -0.5

    def one(qi):
        s = jnp.einsum('bqhd,bkhd->bhqk', qi, k).astype(jnp.float32) * scale
        p = jax.nn.softmax(s, -1).astype(v.dtype)
        return jnp.einsum('bhqk,bkhd->bqhd', p, v)

    o = lax.map(one, qb)
    return o.transpose(1, 0, 2, 3, 4).reshape(B, Lq, H * dv)


def peer(h, w_q, sub_keys, u_tab, v_tab):
    B, L, D = h.shape
    x = h.reshape(-1, D)
    nb = x.shape[0] // PEER_BLOCK
    K = PEER_TOPK

    def block(xb):
        T = xb.shape[0]
        q = (xb @ w_q).reshape(T, PEER_HEADS, 2, PEER_HALF)
        s = jnp.einsum('thpd,hpkd->thpk', q, sub_keys).astype(jnp.float32)
        sv, si = lax.top_k(s, K)
        cand = (sv[:, :, 0, :, None] + sv[:, :, 1, None, :]).reshape(T, PEER_HEADS, K * K)
        cidx = (si[:, :, 0, :, None] * PEER_NKEYS + si[:, :, 1, None, :]).reshape(T, PEER_HEADS, K * K)
        cs, ci = lax.top_k(cand, K)
        eidx = jnp.take_along_axis(cidx, ci, -1)
        gate = jax.nn.softmax(cs, -1)
        act = jax.nn.gelu(jnp.einsum('thkd,td->thk', u_tab[eidx], xb).astype(jnp.float32))
        return jnp.einsum('thk,thkd->td', (gate * act).astype(xb.dtype), v_tab[eidx])

    y = lax.map(block, x.reshape(nb, PEER_BLOCK, D))
    return y.reshape(B, L, D)


def setup_inputs(seed: int = 0) -> dict:
    key = jax.random.key(seed)
    ks = jax.random.split(key, 30)
    f32 = jnp.float32
    D = D_MODEL

    def nrm(k, shape, scale):
        return jax.random.normal(k, shape, f32) * scale

    def gain(k, shape):
        return 1.0 + 0.05 * jax.random.normal(k, shape, f32)

    col_scale = jnp.concatenate([
        jnp.ones((A_CONV + A_V + 2 * GDN_HEADS,), f32),
        jnp.full((2 * GDN_HEADS,), 0.1, f32),
        jnp.ones((B_W,), f32)])
    dt = jnp.exp(jax.random.uniform(ks[14], (N_EVEN, 2, GDN_HEADS), f32,
                                    float(np.log(0.001)), float(np.log(0.1))))
    return {
        'x_prompt': nrm(ks[0], (BATCH, SEQ, D), 1.0),
        'x_sample': nrm(ks[1], (DEC_BATCH, DEC_SEQ, D), 1.0),
        'state_gdn': nrm(ks[2], (DEC_BATCH, N_EVEN, 2, GDN_HEADS, GDN_DK, GDN_DV), GDN_DK ** -0.5),
        'cache_mla_ckv': nrm(ks[3], (DEC_BATCH, N_ODD, PAST_LEN, MLA_KV_LORA), 1.0),
        'cache_mla_krope': nrm(ks[4], (DEC_BATCH, N_ODD, PAST_LEN, MLA_ROPE), 1.0),
        'c': nrm(ks[5], (DEC_BATCH, D), 1.0),
        'c_ctx': nrm(ks[6], (D,), 1.0),
        'w_mod': nrm(ks[7], (DEPTH, D, 6 * D), 0.5 * D ** -0.5),
        'b_mod': nrm(ks[8], (DEPTH, 6 * D), 0.02),
        'norm_mix': gain(ks[9], (DEPTH, D)),
        'norm_ffn': gain(ks[10], (DEPTH, D)),
        'even_w_in': nrm(ks[11], (N_EVEN, D, EVEN_IN), D ** -0.5) * col_scale,
        'even_conv_w': nrm(ks[12], (N_EVEN, GDN_CONV, A_CONV), GDN_CONV ** -0.5),
        'gdn_a_log': jnp.log(jax.random.uniform(ks[13], (N_EVEN, 2, GDN_HEADS), f32, 1.0, 16.0)),
        'gdn_dt_bias': dt + jnp.log(-jnp.expm1(-dt)),
        'gdn_o_norm': gain(ks[15], (N_EVEN, GDN_DV)),
        'even_w_out': nrm(ks[16], (N_EVEN, EVEN_MIX, D), EVEN_MIX ** -0.5),
        'odd_w_in': nrm(ks[17], (N_ODD, D, ODD_IN), D ** -0.5),
        'mla_q_norm': gain(ks[18], (N_ODD, MLA_Q_LORA)),
        'mla_kv_norm': gain(ks[19], (N_ODD, MLA_KV_LORA)),
        'mla_w_uq': nrm(ks[20], (N_ODD, MLA_Q_LORA, MLA_HEADS * MLA_QK), MLA_Q_LORA ** -0.5),
        'mla_w_ukv': nrm(ks[21], (N_ODD, MLA_KV_LORA, MLA_HEADS * (MLA_NOPE + MLA_V)), MLA_KV_LORA ** -0.5),
        'mla_q_headnorm': gain(ks[22], (N_ODD, MLA_QK)),
        'mla_k_headnorm': gain(ks[23], (N_ODD, MLA_QK)),
        'odd_w_out': nrm(ks[24], (N_ODD, MLA_HEADS * MLA_V, D), (MLA_HEADS * MLA_V) ** -0.5),
        'peer_w_q': nrm(ks[25], (DEPTH, D, PEER_HEADS * PEER_DQ), D ** -0.5),
        'peer_sub_keys': nrm(ks[26], (DEPTH, PEER_HEADS, 2, PEER_NKEYS, PEER_HALF), PEER_HALF ** -0.5),
        'peer_u': nrm(ks[27], (DEPTH, PEER_EXPERTS, D), D ** -0.5),
        'peer_v': nrm(ks[28], (DEPTH, PEER_EXPERTS, D), PEER_HEADS ** -0.5),
    }


def reference(x_prompt, x_sample, state_gdn, cache_mla_ckv, cache_mla_krope, c, c_ctx,
              w_mod, b_mod, norm_mix, norm_ffn,
              even_w_in, even_conv_w, gdn_a_log, gdn_dt_bias, gdn_o_norm, even_w_out,
              odd_w_in, mla_q_norm, mla_kv_norm, mla_w_uq, mla_w_ukv, mla_q_headnorm, mla_k_headnorm, odd_w_out,
              peer_w_q, peer_sub_keys, peer_u, peer_v):
    xp, xs = x_prompt, x_sample
    ang_r, ang_c = axial_angles(xs.shape[1])
    c_ctx_b = c_ctx[None, :]
    new_gdn, new_ckv, new_kr = [], [], []
    for l in range(DEPTH):
        j = l // 2
        sh1p, sc1p, g1p, sh2p, sc2p, g2p = ada_mods(c_ctx_b, w_mod[l], b_mod[l])
        sh1s, sc1s, g1s, sh2s, sc2s, g2s = ada_mods(c, w_mod[l], b_mod[l])
        hp = rmsnorm(xp, norm_mix[l]) * (1 + sc1p) + sh1p
        hs = rmsnorm(xs, norm_mix[l]) * (1 + sc1s) + sh1s
        if l % 2 == 0:
            zero = jnp.zeros((xp.shape[0], GDN_HEADS, GDN_DK, GDN_DV), jnp.float32)
            yp, sf, sb = even_mixer(hp, zero, zero, even_w_in[j], even_conv_w[j], gdn_a_log[j],
                                    gdn_dt_bias[j], gdn_o_norm[j], even_w_out[j])
            ys, _, _ = even_mixer(hs, state_gdn[:, j, 0], state_gdn[:, j, 1], even_w_in[j], even_conv_w[j],
                                  gdn_a_log[j], gdn_dt_bias[j], gdn_o_norm[j], even_w_out[j])
            new_gdn.append(jnp.stack([sf, sb], 1).astype(xp.dtype))
        else:
            cq_p, ckv_p, kr_p = mla_compress(hp, odd_w_in[j], mla_q_norm[j], mla_kv_norm[j])
            q_p = mla_queries(cq_p, mla_w_uq[j], mla_q_headnorm[j])
            k_p, v_p = mla_keys_values(ckv_p, kr_p, mla_w_ukv[j], mla_k_headnorm[j])
            yp = attend(q_p, k_p, v_p) @ odd_w_out[j]
            new_ckv.append(ckv_p)
            new_kr.append(kr_p)
            cq_s, ckv_s, kr_s = mla_compress(hs, odd_w_in[j], mla_q_norm[j], mla_kv_norm[j])
            q_s = rope_latent(mla_queries(cq_s, mla_w_uq[j], mla_q_headnorm[j]), ang_r, ang_c)
            k_s, v_s = mla_keys_values(ckv_s, kr_s, mla_w_ukv[j], mla_k_headnorm[j])
            k_s = rope_latent(k_s, ang_r, ang_c)
            k_c, v_c = mla_keys_values(cache_mla_ckv[:, j], cache_mla_krope[:, j], mla_w_ukv[j], mla_k_headnorm[j])
            ys = attend(q_s, jnp.concatenate([k_s, k_c], 1), jnp.concatenate([v_s, v_c], 1)) @ odd_w_out[j]
        xp = xp + g1p * yp
        xs = xs + g1s * ys
        hp = rmsnorm(xp, norm_ffn[l]) * (1 + sc2p) + sh2p
        hs = rmsnorm(xs, norm_ffn[l]) * (1 + sc2s) + sh2s
        xp = xp + g2p * peer(hp, peer_w_q[l], peer_sub_keys[l], peer_u[l], peer_v[l])
        xs = xs + g2s * peer(hs, peer_w_q[l], peer_sub_keys[l], peer_u[l], peer_v[l])
    new_state_gdn = jnp.stack(new_gdn, 1)
    new_mla_ckv = jnp.stack(new_ckv, 1)
    new_mla_krope = jnp.stack(new_kr, 1)
    return (xp, xs, new_state_gdn, new_mla_ckv, new_mla_krope)
```

```python
from contextlib import ExitStack
import numpy as np
import concourse.bass as bass
import concourse.mybir as mybir
from concourse.bass_utils import run_bass_kernel_spmd

F32 = mybir.dt.float32
I32 = mybir.dt.int32
U32 = mybir.dt.uint32
ALU = mybir.AluOpType
AF = mybir.ActivationFunctionType
AX = mybir.AxisListType

D = 1024
EPS = 1e-6
NEG = -1.0e30


class Sem:
    def __init__(self, h):
        self.h = h
        self.count = 0


class Buf:
    def __init__(self, ap=None, name=""):
        self.ap = ap
        self.name = name
        self.w = {}
        self.r = {}
        self.pre = {}
        self.dsem = None

    def __getitem__(self, k):
        return self.ap[k]


class K:
    ENG = ("pe", "act", "dve", "pool", "sp")

    def __init__(self, nc, es):
        self.nc = nc
        self.es = es
        self.q = {e: [] for e in self.ENG}
        self.known = {e: {} for e in self.ENG}
        self.es0 = es
        self.all_sems = []
        self.free_sems = []
        self.phase_bufs = []
        self.esem = {e: self.sem("s_" + e) for e in self.ENG}
        self.n_ops = 0
        self.nsem = 0

    def sem(self, name):
        sm = Sem(self.es0.enter_context(self.nc.semaphore(name)))
        self.all_sems.append(sm)
        return sm

    def barrier(self):
        snap = Buf(None, "bar")
        snap.w = {sm: sm.count for sm in self.all_sems if sm.count > 0}
        for e in self.ENG:
            self.op(e, lambda en: en.nop(), rd=[snap])

    def phase_begin(self):
        self.es_saved = self.es
        self.es = ExitStack()
        self.phase_bufs = []

    def phase_end(self):
        self.barrier()
        for b in self.phase_bufs:
            if b.dsem is not None:
                self.free_sems.append(b.dsem)
                b.dsem = None
        self.phase_bufs = []
        self.es.close()
        self.es = self.es_saved

    def sb(self, name, shape, dt=F32):
        self.uid = getattr(self, "uid", 0) + 1
        name = "%s_%d" % (name, self.uid)
        t = self.es.enter_context(self.nc.sbuf_tensor(name, list(shape), dt))
        b = Buf(t, name)
        self.phase_bufs.append(b)
        return b

    def ps(self, name, shape, dt=F32):
        t = self.es.enter_context(self.nc.psum_tensor(name, list(shape), dt))
        return Buf(t, name)

    def dram(self, name, shape, dt=F32, kind="Internal"):
        t = self.nc.dram_tensor(name, list(shape), dt, kind=kind).ap()
        return Buf(t, name)

    def op(self, eng, fn, rd=(), wr=(), acc=(), dsem=None):
        need = {}

        def merge(d):
            for s, v in d.items():
                if need.get(s, 0) < v:
                    need[s] = v

        for b in rd:
            merge(b.w)
        for b in wr:
            merge(b.w)
            merge(b.r)
        for b in acc:
            merge(b.r)
            merge(b.pre)
        kn = self.known[eng]
        waits = []
        for s, v in need.items():
            if eng == "pe" and s is self.esem["pe"]:
                continue
            if kn.get(s, 0) < v:
                waits.append((s, v))
                kn[s] = v
        if dsem is not None:
            dsem.count += 16
            tok = (dsem, dsem.count)
            inc = (dsem, 16)
        else:
            s = self.esem[eng]
            s.count += 1
            tok = (s, s.count)
            inc = (s, 1)
        self.q[eng].append((waits, fn, inc))
        self.n_ops += 1 + len(waits)
        for b in rd:
            if b.r.get(tok[0], 0) < tok[1]:
                b.r[tok[0]] = tok[1]
        for b in wr:
            pre = dict(b.w)
            for s_, v_ in b.r.items():
                if pre.get(s_, 0) < v_:
                    pre[s_] = v_
            b.pre = pre
            b.w = {tok[0]: tok[1]}
            b.r = {}
        for b in acc:
            b.w[tok[0]] = tok[1]
        return tok

    def _dsem(self, b):
        if b.dsem is None:
            if self.free_sems:
                b.dsem = self.free_sems.pop()
            else:
                self.nsem += 1
                b.dsem = self.sem("d%d" % self.nsem)
        return b.dsem

    def load(self, sbuf, out_ap, dbuf, in_ap, eng="sp", acc=False, **kw):
        ds = self._dsem(sbuf)
        f = lambda e: e.dma_start(out=out_ap, in_=in_ap, **kw)
        if acc:
            return self.op(eng, f, rd=[dbuf], acc=[sbuf], dsem=ds)
        return self.op(eng, f, rd=[dbuf], wr=[sbuf], dsem=ds)

    def store(self, dbuf, out_ap, sbuf, in_ap, eng="sp", full=False, **kw):
        ds = self._dsem(sbuf)
        f = lambda e: e.dma_start(out=out_ap, in_=in_ap, **kw)
        if full:
            return self.op(eng, f, rd=[sbuf], wr=[dbuf], dsem=ds)
        return self.op(eng, f, rd=[sbuf], acc=[dbuf], dsem=ds)

    def mm(self, ps, out_ap, lhsT, rhs, start, stop, rd=()):
        f = lambda e: e.matmul(out_ap, lhsT, rhs, start=start, stop=stop)
        if start:
            return self.op("pe", f, rd=rd, wr=[ps])
        return self.op("pe", f, rd=rd, acc=[ps])

    def tr(self, ps, out_ap, in_ap, ident, rd=(), first=True):
        f = lambda e: e.transpose(out_ap, in_ap, ident)
        if first:
            return self.op("pe", f, rd=rd, wr=[ps])
        return self.op("pe", f, rd=rd, acc=[ps])

    def emit(self):
        nc = self.nc
        q = self.q
        with nc.Block() as block:
            def run(eng, name):
                for waits, fn, inc in q[name]:
                    for s, v in waits:
                        eng.wait_ge(s.h, v)
                    ins = fn(eng)
                    ins.then_inc(inc[0].h, inc[1])

            @block.tensor
            def _(e):
                run(e, "pe")

            @block.scalar
            def _(e):
                run(e, "act")

            @block.vector
            def _(e):
                run(e, "dve")

            @block.gpsimd
            def _(e):
                run(e, "pool")

            @block.sync
            def _(e):
                run(e, "sp")


def bc(ap, shape):
    return ap.to_broadcast(list(shape))


def peer_phase(k, C, l, xres, T, tok_groups):
    nc = k.nc
    ident = C["ident"]
    NT = T // 128
    Hd = C["H_dram"]
    Sd = C["S_dram"]
    if True:
        k.phase_begin()
        wq = k.sb("wq", [128, 8, 2048])
        k.load(wq, wq[:], C["peer_w_q"], C["peer_w_q"][l].rearrange("(c p) n -> p c n", p=128))
        skl = k.sb("skl", [128, 16, 128])
        k.load(skl, skl[:], C["peer_sub_keys"],
               C["peer_sub_keys"][l].rearrange("h t k d -> k (h t) d"))
        KT = k.sb("KT", [128, 16, 128])
        pst = C["ps"]
        for g in range(4):
            p = pst[g % 2]
            for j in range(4):
                gi = g * 4 + j
                k.tr(p, p[:, j * 128:(j + 1) * 128], skl[:, gi, :], ident[:], rd=[skl, ident], first=(j == 0))
            k.op("act", lambda e, p=p, g=g: e.activation(
                out=KT[:, g * 4:(g + 1) * 4, :], in_=p[:].rearrange("p (a b) -> p a b", a=4), func=AF.Copy),
                rd=[p], acc=[KT])
        xt = [k.sb("xA%d" % i, [128, D]) for i in range(1)]
        ht = [k.sb("hA%d" % i, [128, D]) for i in range(2)]
        junkA_ = k.sb("junkA", [128, D])
        st = [k.sb("stA%d" % i, [128, 4]) for i in range(2)]
        hT = k.sb("hTA", [128, 8, 512])
        qT = [k.sb("qTA%d" % i, [128, 512]) for i in range(2)]
        ssb = k.sb("ssbA", [128, 4, 16, 128])
        ti = 0
        for (t0, ntl, ms) in tok_groups:
            a2, sh2 = C["mods"][ms][3], C["mods"][ms][4]
            for gs in range(t0, t0 + ntl, 4):
                gn = min(4, t0 + ntl - gs)
                for j in range(gn):
                    tile = gs + j
                    x = xt[0]
                    h = ht[ti % 2]
                    s_ = st[ti % 2]
                    ti += 1
                    k.load(x, x[:], xres, xres[tile * 128:(tile + 1) * 128, :])
                    k.op("dve", lambda e, x=x, s_=s_: e.scalar_tensor_tensor(
                        out=junkA_[:], in0=x[:], scalar=1.0, in1=x[:], op0=ALU.mult, op1=ALU.mult,
                        accum_out=s_[:, 0:1]), rd=[x], wr=[junkA_, s_])
                    k.op("dve", lambda e, s_=s_: e.tensor_scalar(
                        out=s_[:, 1:2], in0=s_[:, 0:1], scalar1=1.0 / D, scalar2=EPS, op0=ALU.mult, op1=ALU.add),
                        rd=[s_], acc=[s_])
                    k.op("act", lambda e, s_=s_: e.activation(out=s_[:, 2:3], in_=s_[:, 1:2], func=AF.Sqrt),
                         rd=[s_], acc=[s_])
                    k.op("dve", lambda e, s_=s_: e.reciprocal(out=s_[:, 3:4], in_=s_[:, 2:3]), rd=[s_], acc=[s_])
                    k.op("dve", lambda e, x=x, h=h, s_=s_, a2=a2: e.scalar_tensor_tensor(
                        out=h[:], in0=x[:], scalar=s_[:, 3:4], in1=a2[:], op0=ALU.mult, op1=ALU.mult),
                        rd=[x, s_, a2], wr=[h])
                    k.op("dve", lambda e, h=h, sh2=sh2: e.tensor_tensor(out=h[:], in0=h[:], in1=sh2[:], op=ALU.add),
                         rd=[h, sh2], wr=[h])
                    k.store(Hd, Hd[tile * 128:(tile + 1) * 128, :], h, h[:])
                    to_hT(k, C, h, hT, j * 128, False)
                ncol = gn * 128
                if "dbg_hT" in C:
                    k.store(C["dbg_hT"], C["dbg_hT"][:, :], hT, hT[:].rearrange("p a b -> p (a b)"))
                for gi in range(16):
                    pq = pst[2 + gi % 2]
                    for kc in range(8):
                        k.mm(pq, pq[:, :ncol], wq[:, kc, gi * 128:(gi + 1) * 128], hT[:, kc, :ncol],
                             start=(kc == 0), stop=(kc == 7), rd=[wq, hT])
                    q = qT[gi % 2]
                    k.op("act", lambda e, q=q, pq=pq, ncol=ncol: e.activation(out=q[:, :ncol], in_=pq[:, :ncol], func=AF.Copy),
                         rd=[pq], wr=[q])
                    for j in range(gn):
                        psc = pst[4]
                        k.op("pe", lambda e, psc=psc, q=q, j=j, gi=gi: e.matmul(
                            psc[:, 0:128], q[:, j * 128:(j + 1) * 128], KT[:, gi, :],
                            start=True, stop=True), rd=[q, KT], wr=[psc])
                        k.op("dve" if j % 2 else "act", (lambda e, psc=psc, gi=gi, j=j: e.tensor_copy(
                            out=ssb[:, j, gi, :], in_=psc[:, 0:128])) if j % 2 else
                            (lambda e, psc=psc, gi=gi, j=j: e.activation(out=ssb[:, j, gi, :], in_=psc[:, 0:128], func=AF.Copy)),
                            rd=[psc], acc=[ssb] if (gi or j) else (), wr=() if (gi or j) else [ssb])
                for j in range(gn):
                    tile = gs + j
                    k.store(Sd, Sd[tile * 128:(tile + 1) * 128, :], ssb,
                            ssb[:, j, :, :].rearrange("p a b -> p (a b)"))
        k.phase_end()
    if True:
        k.phase_begin()
        NR = 8
        ub = [k.sb("ub%d" % i, [128, D]) for i in range(NR)]
        vb = [k.sb("vb%d" % i, [128, D]) for i in range(NR)]
        xt = [k.sb("xB%d" % i, [128, D]) for i in range(2)]
        ht = [k.sb("hB%d" % i, [128, D]) for i in range(2)]
        sv_ = [k.sb("sB%d" % i, [128, 16, 128]) for i in range(2)]
        s2 = k.sb("s2B", [128, 16, 128])
        junk = k.sb("junkB", [128, D])
        accb = k.sb("accB", [128, D])
        sv = k.sb("svB", [128, 16, 16])
        si = k.sb("siB", [128, 16, 16], U32)
        sif = k.sb("sifB", [128, 16, 16])
        cand = k.sb("candB", [128, 8, 256])
        cand2 = k.sb("cand2B", [128, 8, 256])
        cs = k.sb("csB", [128, 8, 16])
        ci = k.sb("ciB", [128, 8, 16], U32)
        ca = k.sb("caB", [128, 8, 16], U32)
        cb = k.sb("cbB", [128, 8, 16], U32)
        caf = k.sb("cafB", [128, 8, 16])
        cbf = k.sb("cbfB", [128, 8, 16])
        oh = k.sb("ohB", [128, 8, 16, 16])
        ei = k.sb("eiB", [128, 8, 16])
        ej = k.sb("ejB", [128, 8, 16])
        ef = k.sb("efB", [128, 128])
        eidx = [k.sb("eidxB%d" % i, [128, 128], I32) for i in range(2)]
        gt = k.sb("gateB", [128, 8, 16])
        zz = k.sb("zzB", [128, 8, 2])
        actb = k.sb("actB", [128, 128])
        wgt = k.sb("wgtB", [128, 128])
        tmp = k.sb("tmpB", [128, 128])
        iota16 = C["iota16"]
        PU, PV = C["peer_u"], C["peer_v"]
        ti = 0
        for (t0, ntl, ms) in tok_groups:
            g2 = C["mods"][ms][5]
            for tile in range(t0, t0 + ntl):
                x = xt[ti % 2]
                h = ht[ti % 2]
                s_ = sv_[ti % 2]
                ex = eidx[ti % 2]
                ti += 1
                rows = slice(tile * 128, (tile + 1) * 128)
                k.load(s_, s_[:].rearrange("p a b -> p (a b)"), Sd, Sd[rows, :])
                k.load(h, h[:], Hd, Hd[rows, :])
                k.load(x, x[:], xres, xres[rows, :])
                for g in range(16):
                    k.op("dve", lambda e, g=g, s_=s_: e.max(out=sv[:, g, 0:8], in_=s_[:, g, :]),
                         rd=[s_], wr=[sv] if g == 0 else (), acc=() if g == 0 else [sv])
                    k.op("dve", lambda e, g=g, s_=s_: e.max_index(out=si[:, g, 0:8], in_max=sv[:, g, 0:8], in_values=s_[:, g, :]),
                         rd=[s_, sv], wr=[si] if g == 0 else (), acc=() if g == 0 else [si])
                    k.op("dve", lambda e, g=g, s_=s_: e.match_replace(
                        out=s2[:, g, :], in_to_replace=sv[:, g, 0:8], in_values=s_[:, g, :], imm_value=NEG),
                        rd=[s_, sv], wr=[s2] if g == 0 else (), acc=() if g == 0 else [s2])
                    k.op("dve", lambda e, g=g: e.max(out=sv[:, g, 8:16], in_=s2[:, g, :]), rd=[s2], acc=[sv])
                    k.op("dve", lambda e, g=g: e.max_index(out=si[:, g, 8:16], in_max=sv[:, g, 8:16], in_values=s2[:, g, :]),
                         rd=[s2, sv], acc=[si])
                svr = sv[:].rearrange("p (h t) k -> p h t k", t=2)
                k.op("dve", lambda e, svr=svr: e.tensor_tensor(
                    out=cand[:].rearrange("p h (a b) -> p h a b", a=16),
                    in0=bc(svr[:, :, 0, :].unsqueeze(3), [128, 8, 16, 16]),
                    in1=bc(svr[:, :, 1, :].unsqueeze(2), [128, 8, 16, 16]), op=ALU.add), rd=[sv], wr=[cand])
                for hh in range(8):
                    k.op("dve", lambda e, hh=hh: e.max(out=cs[:, hh, 0:8], in_=cand[:, hh, :]),
                         rd=[cand], wr=[cs] if hh == 0 else (), acc=() if hh == 0 else [cs])
                    k.op("dve", lambda e, hh=hh: e.max_index(out=ci[:, hh, 0:8], in_max=cs[:, hh, 0:8], in_values=cand[:, hh, :]),
                         rd=[cand, cs], wr=[ci] if hh == 0 else (), acc=() if hh == 0 else [ci])
                    k.op("dve", lambda e, hh=hh: e.match_replace(
                        out=cand2[:, hh, :], in_to_replace=cs[:, hh, 0:8], in_values=cand[:, hh, :], imm_value=NEG),
                        rd=[cand, cs], wr=[cand2] if hh == 0 else (), acc=() if hh == 0 else [cand2])
                    k.op("dve", lambda e, hh=hh: e.max(out=cs[:, hh, 8:16], in_=cand2[:, hh, :]), rd=[cand2], acc=[cs])
                    k.op("dve", lambda e, hh=hh: e.max_index(out=ci[:, hh, 8:16], in_max=cs[:, hh, 8:16], in_values=cand2[:, hh, :]),
                         rd=[cand2, cs], acc=[ci])
                k.op("dve", lambda e: e.tensor_single_scalar(out=ca[:], in_=ci[:], scalar=4, op=ALU.logical_shift_right),
                     rd=[ci], wr=[ca])
                k.op("dve", lambda e: e.tensor_single_scalar(out=cb[:], in_=ci[:], scalar=15, op=ALU.bitwise_and),
                     rd=[ci], wr=[cb])
                k.op("dve", lambda e: e.tensor_copy(out=caf[:], in_=ca[:]), rd=[ca], wr=[caf])
                k.op("dve", lambda e: e.tensor_copy(out=cbf[:], in_=cb[:]), rd=[cb], wr=[cbf])
                k.op("dve", lambda e: e.tensor_copy(out=sif[:], in_=si[:]), rd=[si], wr=[sif])
                sifr = sif[:].rearrange("p (h t) k -> p h t k", t=2)
                for (cf, side, eo) in ((caf, 0, ei), (cbf, 1, ej)):
                    k.op("dve", lambda e, cf=cf: e.tensor_tensor(
                        out=oh[:], in0=bc(cf[:].unsqueeze(3), [128, 8, 16, 16]),
                        in1=bc(iota16[:].unsqueeze(1).unsqueeze(1), [128, 8, 16, 16]), op=ALU.is_equal),
                        rd=[cf, iota16], wr=[oh])
                    k.op("dve", lambda e, side=side, sifr=sifr: e.tensor_tensor(
                        out=oh[:], in0=oh[:], in1=bc(sifr[:, :, side, :].unsqueeze(2), [128, 8, 16, 16]), op=ALU.mult),
                        rd=[oh, sif], wr=[oh])
                    k.op("dve", lambda e, eo=eo: e.tensor_reduce(out=eo[:], in_=oh[:], axis=AX.X, op=ALU.add),
                         rd=[oh], wr=[eo])
                k.op("dve", lambda e: e.scalar_tensor_tensor(
                    out=ef[:], in0=ei[:].rearrange("p h k -> p (h k)"), scalar=128.0,
                    in1=ej[:].rearrange("p h k -> p (h k)"), op0=ALU.mult, op1=ALU.add), rd=[ei, ej], wr=[ef])
                k.op("dve", lambda e, ex=ex: e.tensor_copy(out=ex[:], in_=ef[:]), rd=[ef], wr=[ex])
                k.op("dve", lambda e: e.tensor_tensor(
                    out=gt[:], in0=cs[:], in1=bc(cs[:, :, 0:1], [128, 8, 16]), op=ALU.subtract), rd=[cs], wr=[gt])
                k.op("act", lambda e: e.activation(out=gt[:], in_=gt[:], func=AF.Exp), rd=[gt], wr=[gt])
                k.op("dve", lambda e: e.tensor_reduce(out=zz[:, :, 0], in_=gt[:], axis=AX.X, op=ALU.add),
                     rd=[gt], wr=[zz])
                k.op("dve", lambda e: e.reciprocal(out=zz[:, :, 1], in_=zz[:, :, 0]), rd=[zz], acc=[zz])
                k.op("dve", lambda e: e.tensor_tensor(
                    out=gt[:], in0=gt[:], in1=bc(zz[:, :, 1:2], [128, 8, 16]), op=ALU.mult), rd=[gt, zz], wr=[gt])
                for j in range(128):
                    u = ub[j % NR]
                    k.op("pool", lambda e, u=u, ex=ex, j=j: e.indirect_dma_start(
                        out=u[:], out_offset=None, in_=PU[:].rearrange("l e d -> (l e) d"),
                        in_offset=bass.IndirectOffsetOnAxis(ap=ex[:, j:j + 1], axis=0),
                        element_offset=l * 16384 * D),
                        rd=[ex, PU], wr=[u], dsem=k._dsem(u))
                    k.op("dve", lambda e, u=u, h=h, j=j: e.scalar_tensor_tensor(
                        out=junk[:], in0=u[:], scalar=1.0, in1=h[:], op0=ALU.mult, op1=ALU.mult,
                        accum_out=actb[:, j:j + 1]), rd=[u, h], wr=[junk, actb] if j == 0 else [junk], acc=[actb] if j else ())
                k.op("dve", lambda e: e.tensor_tensor(out=tmp[:], in0=actb[:], in1=actb[:], op=ALU.mult), rd=[actb], wr=[tmp])
                k.op("dve", lambda e: e.tensor_scalar(out=tmp[:], in0=tmp[:], scalar1=0.044715, scalar2=1.0,
                                                      op0=ALU.mult, op1=ALU.add), rd=[tmp], wr=[tmp])
                k.op("dve", lambda e: e.tensor_tensor(out=tmp[:], in0=tmp[:], in1=actb[:], op=ALU.mult), rd=[tmp, actb], wr=[tmp])
                k.op("act", lambda e: e.activation(out=tmp[:], in_=tmp[:], func=AF.Sigmoid, scale=1.5957691216057308),
                     rd=[tmp], wr=[tmp])
                k.op("dve", lambda e: e.tensor_tensor(out=wgt[:], in0=tmp[:], in1=actb[:], op=ALU.mult), rd=[tmp, actb], wr=[wgt])
                k.op("dve", lambda e: e.tensor_tensor(out=wgt[:], in0=wgt[:], in1=gt[:].rearrange("p h k -> p (h k)"), op=ALU.mult),
                     rd=[wgt, gt], wr=[wgt])
                for j in range(128):
                    v = vb[j % NR]
                    k.op("pool", lambda e, v=v, ex=ex, j=j: e.indirect_dma_start(
                        out=v[:], out_offset=None, in_=PV[:].rearrange("l e d -> (l e) d"),
                        in_offset=bass.IndirectOffsetOnAxis(ap=ex[:, j:j + 1], axis=0),
                        element_offset=l * 16384 * D),
                        rd=[ex, PV], wr=[v], dsem=k._dsem(v))
                    if j == 0:
                        k.op("dve", lambda e, v=v: e.tensor_scalar(
                            out=accb[:], in0=v[:], scalar1=wgt[:, 0:1], scalar2=None, op0=ALU.mult),
                            rd=[v, wgt], wr=[accb])
                    else:
                        k.op("dve", lambda e, v=v, j=j: e.scalar_tensor_tensor(
                            out=accb[:], in0=v[:], scalar=wgt[:, j:j + 1], in1=accb[:], op0=ALU.mult, op1=ALU.add),
                            rd=[v, wgt, accb], wr=[accb])
                k.op("dve", lambda e, g2=g2: e.tensor_tensor(out=accb[:], in0=accb[:], in1=g2[:], op=ALU.mult),
                     rd=[accb, g2], wr=[accb])
                k.op("dve", lambda e, x=x: e.tensor_tensor(out=x[:], in0=x[:], in1=accb[:], op=ALU.add),
                     rd=[accb, x], wr=[x])
                k.store(xres, xres[rows, :], x, x[:])
        k.phase_end()


def nps(C):
    C["pi"] = C.get("pi", 0) + 1
    return C["ps"][C["pi"] % 8]


def evac(k, eng, out_ap, in_ap, rd, wr=(), acc=(), scale=None):
    if eng == "act":
        if scale is None:
            fn = lambda e: e.activation(out=out_ap, in_=in_ap, func=AF.Copy)
        else:
            fn = lambda e: e.activation(out=out_ap, in_=in_ap, func=AF.Copy, scale=scale)
    else:
        fn = lambda e: e.tensor_copy(out=out_ap, in_=in_ap)
    return k.op(eng, fn, rd=rd, wr=wr, acc=acc)


def rstd_from_ss(k, st, div, eps=EPS):
    k.op("dve", lambda e: e.tensor_scalar(out=st[:, 1:2], in0=st[:, 0:1], scalar1=1.0 / div, scalar2=eps,
                                          op0=ALU.mult, op1=ALU.add), rd=[st], acc=[st])
    k.op("act", lambda e: e.activation(out=st[:, 2:3], in_=st[:, 1:2], func=AF.Sqrt), rd=[st], acc=[st])
    k.op("dve", lambda e: e.reciprocal(out=st[:, 3:4], in_=st[:, 2:3]), rd=[st], acc=[st])


def front(k, x, h, st, junk, a, sh):
    k.op("act", lambda e: e.activation(out=junk[:, :D], in_=x[:], func=AF.Square, accum_out=st[:, 0:1]),
         rd=[x], wr=[junk, st])
    rstd_from_ss(k, st, D)
    k.op("dve", lambda e: e.scalar_tensor_tensor(out=h[:], in0=x[:], scalar=st[:, 3:4], in1=a[:],
                                                 op0=ALU.mult, op1=ALU.mult), rd=[x, st, a], wr=[h])
    k.op("pool", lambda e: e.tensor_tensor(out=h[:], in0=h[:], in1=sh[:], op=ALU.add), rd=[h, sh], wr=[h])


def tp_group(k, C, srcs, src_bufs, dst_ap, dst_buf, first, w=128, eng="act"):
    p = nps(C)
    ident = C["ident"]
    n = len(srcs)
    for j, s_ in enumerate(srcs):
        k.tr(p, p[:w, j * 128:(j + 1) * 128], s_, ident[:], rd=list(src_bufs) + [ident], first=(j == 0))
    evac(k, eng, dst_ap, p[:w, :n * 128].rearrange("p (a b) -> p a b", a=n), rd=[p],
         wr=[dst_buf] if first else (), acc=() if first else [dst_buf])


def to_hT(k, C, h, hT, col0, first):
    for half in range(2):
        tp_group(k, C, [h[:, (half * 4 + c) * 128:(half * 4 + c + 1) * 128] for c in range(4)], [h],
                 hT[:, half * 4:(half + 1) * 4, col0:col0 + 128], hT, first and half == 0,
                 eng="act" if half else "dve")


def rep_rows(k, C, rowbuf, c0, n, dst_buf, d0=0, scale=None):
    ones1 = C["ones1"]
    for o in range(0, n, 512):
        w = min(512, n - o)
        p = nps(C)
        k.mm(p, p[:, :w], ones1[0:1, :], rowbuf[0:1, c0 + o:c0 + o + w], True, True, rd=[ones1, rowbuf])
        evac(k, "act", dst_buf[:, d0 + o:d0 + o + w], p[:, :w], rd=[p], acc=[dst_buf], scale=scale)


def mods_setup(k, C):
    cin = C["cvec"]
    cT = k.sb("cT", [128, 2, 8])
    for ms in range(2):
        k.load(cT, cT[:, ms, :], cin, cin[ms, :].rearrange("(c p) -> p c", p=128), acc=(ms > 0),
               allow_slow_non_contiguous=True)
    k.op("act", lambda e: e.activation(out=cT[:], in_=cT[:], func=AF.Silu), rd=[cT], wr=[cT])
    C["rep"] = []
    for ms in range(2):
        r = k.sb("crep%d" % ms, [128, 8, 128])
        k.op("dve", lambda e, r=r, ms=ms: e.tensor_copy(out=r[:], in_=bc(cT[:, ms, :].unsqueeze(2), [128, 8, 128])),
             rd=[cT], wr=[r])
        C["rep"].append(r)
    C["mods"] = [[k.sb("mod%d_%d" % (ms, i), [128, D]) for i in range(6)] for ms in range(2)]


def mods_phase(k, C, l):
    k.phase_begin()
    rows = k.sb("mrows", [1, 8192])
    k.load(rows, rows[0:1, 0:6144], C["b_mod"], C["b_mod"][l:l + 1, :])
    k.load(rows, rows[0:1, 6144:7168], C["norm_mix"], C["norm_mix"][l:l + 1, :], acc=True)
    k.load(rows, rows[0:1, 7168:8192], C["norm_ffn"], C["norm_ffn"][l:l + 1, :], acc=True)
    wb = [k.sb("wmb%d" % i, [128, 8, 512]) for i in range(2)]
    dst = [1, 0, 2, 4, 3, 5]
    ones1 = C["ones1"]
    wm = C["w_mod"]
    for blk in range(12):
        w = wb[blk % 2]
        k.load(w, w[:], wm, wm[l][:, blk * 512:(blk + 1) * 512].rearrange("(c p) n -> p c n", p=128))
        for ms in range(2):
            p = nps(C)
            for kc in range(8):
                k.mm(p, p[:], C["rep"][ms][:, kc, :], w[:, kc, :], kc == 0, False, rd=[C["rep"][ms], w])
            k.mm(p, p[:], ones1[0:1, :], rows[0:1, blk * 512:(blk + 1) * 512], False, True, rd=[ones1, rows])
            m = C["mods"][ms][dst[blk // 2]]
            half = blk % 2
            evac(k, "act" if ms else "dve", m[:, half * 512:(half + 1) * 512], p[:], rd=[p],
                 wr=[m] if half == 0 else (), acc=() if half == 0 else [m])
    for off, idx in ((6144, 0), (7168, 3)):
        for half in range(2):
            p = nps(C)
            k.mm(p, p[:], ones1[0:1, :], rows[0:1, off + half * 512:off + (half + 1) * 512], True, True,
                 rd=[ones1, rows])
            for ms in range(2):
                m = C["mods"][ms][idx]
                k.op("dve", lambda e, m=m, p=p, half=half: e.scalar_tensor_tensor(
                    out=m[:, half * 512:(half + 1) * 512], in0=m[:, half * 512:(half + 1) * 512], scalar=1.0,
                    in1=p[:], op0=ALU.add, op1=ALU.mult), rd=[m, p], wr=[m] if half == 1 else (),
                    acc=[m] if half == 0 else ())
    k.phase_end()


def outproj_phase(k, C, xres, wout_ap, wbuf, Mix_d, tok_groups, pre=None, pre_alloc=None):
    k.phase_begin()
    wo = k.sb("wo", [128, 8, D])
    k.load(wo, wo[:], wbuf, wout_ap.rearrange("(c p) n -> p c n", p=128))
    mt = [k.sb("mxt%d" % i, [128, D]) for i in range(2)]
    xt = [k.sb("xo%d" % i, [128, D]) for i in range(2)]
    mT = k.sb("mTo", [128, 8, 128])
    tmp = k.sb("tmpo", [128, D])
    aux = pre_alloc(k) if pre_alloc else None
    ti = 0
    for (t0, ntl, ms) in tok_groups:
        g1 = C["mods"][ms][2]
        for tile in range(t0, t0 + ntl):
            m = mt[ti % 2]
            x = xt[ti % 2]
            ti += 1
            rows = slice(tile * 128, (tile + 1) * 128)
            k.load(m, m[:], Mix_d, Mix_d[rows, :])
            k.load(x, x[:], xres, xres[rows, :])
            if pre:
                pre(k, aux, tile, m)
            to_hT(k, C, m, mT, 0, True)
            for half in range(2):
                p = nps(C)
                for kc in range(8):
                    k.mm(p, p[:], mT[:, kc, :], wo[:, kc, half * 512:(half + 1) * 512], kc == 0, kc == 7,
                         rd=[mT, wo])
                k.op("dve", lambda e, p=p, half=half, g1=g1: e.tensor_tensor(
                    out=tmp[:, half * 512:(half + 1) * 512], in0=p[:], in1=g1[:, half * 512:(half + 1) * 512],
                    op=ALU.mult), rd=[p, g1], wr=[tmp] if half == 0 else (), acc=() if half == 0 else [tmp])
            k.op("pool", lambda e, x=x: e.tensor_tensor(out=x[:], in0=x[:], in1=tmp[:], op=ALU.add),
                 rd=[x, tmp], wr=[x])
            k.store(xres, xres[rows, :], x, x[:])
    k.phase_end()


def rope(k, R, Rbuf, rt, tb):
    R5 = R.rearrange("p h (a b c) -> p h a b c", a=2, b=2)
    x1, x2 = R5[:, :, :, 0, :], R5[:, :, :, 1, :]
    rtv = rt[:].rearrange("p (s a c) -> p s a c", s=2, a=2)
    cos = bc(rtv[:, 0].unsqueeze(1), [128, 8, 2, 16])
    sin = bc(rtv[:, 1].unsqueeze(1), [128, 8, 2, 16])
    t = [tb[:, i].rearrange("p (h a c) -> p h a c", h=8, a=2) for i in range(4)]
    k.op("dve", lambda e: e.tensor_tensor(out=t[0], in0=x1, in1=cos, op=ALU.mult), rd=[Rbuf, rt], wr=[tb])
    k.op("dve", lambda e: e.tensor_tensor(out=t[1], in0=x2, in1=sin, op=ALU.mult), rd=[Rbuf, rt], acc=[tb])
    k.op("pool", lambda e: e.tensor_tensor(out=t[2], in0=x1, in1=sin, op=ALU.mult), rd=[Rbuf, rt], acc=[tb])
    k.op("pool", lambda e: e.tensor_tensor(out=t[3], in0=x2, in1=cos, op=ALU.mult), rd=[Rbuf, rt], acc=[tb])
    k.op("dve", lambda e: e.tensor_tensor(out=x1, in0=t[0], in1=t[1], op=ALU.subtract), rd=[tb, Rbuf], wr=[Rbuf])
    k.op("dve", lambda e: e.tensor_tensor(out=x2, in0=t[2], in1=t[3], op=ALU.add), rd=[tb], acc=[Rbuf])


def mla_phase(k, C, l, xres, seqs):
    j = l // 2
    QTn, QTr, KTn, KTr, Vp, Od = C["QTn"], C["QTr"], C["KTn"], C["KTr"], C["Vp"], C["O_d"]
    k.phase_begin()
    win = k.sb("mwin", [128, 8, 832])
    k.load(win, win[:], C["odd_w_in"], C["odd_w_in"][j].rearrange("(c p) n -> p c n", p=128))
    wuq = k.sb("mwuq", [128, 4, 1536])
    k.load(wuq, wuq[:], C["mla_w_uq"], C["mla_w_uq"][j].rearrange("(c p) n -> p c n", p=128))
    wukv = k.sb("mwukv", [128, 2, 2048])
    k.load(wukv, wukv[:], C["mla_w_ukv"], C["mla_w_ukv"][j].rearrange("(c p) n -> p c n", p=128))
    rows = k.sb("mrowsA", [1, 1152])
    k.load(rows, rows[0:1, 0:512], C["mla_q_norm"], C["mla_q_norm"][j:j + 1, :])
    k.load(rows, rows[0:1, 512:768], C["mla_kv_norm"], C["mla_kv_norm"][j:j + 1, :], acc=True)
    k.load(rows, rows[0:1, 768:960], C["mla_q_headnorm"], C["mla_q_headnorm"][j:j + 1, :], acc=True)
    k.load(rows, rows[0:1, 960:1152], C["mla_k_headnorm"], C["mla_k_headnorm"][j:j + 1, :], acc=True)
    gains = k.sb("mgains", [128, 1152])
    rep_rows(k, C, rows, 0, 768, gains, 0)
    rep_rows(k, C, rows, 768, 192, gains, 768, scale=192.0 ** -0.5)
    rep_rows(k, C, rows, 960, 192, gains, 960)
    xt = [k.sb("mx%d" % i, [128, D]) for i in range(1)]
    h = k.sb("mh", [128, D])
    st = k.sb("mst", [128, 4])
    st2 = k.sb("mst2", [128, 4])
    hT = k.sb("mhT", [128, 8, 128])
    psb = [k.sb("mpsb%d" % i, [128, 832]) for i in range(1)]
    cq = k.sb("mcq", [128, 512])
    ckv = [k.sb("mckv%d" % i, [128, 256]) for i in range(1)]
    cqT = k.sb("mcqT", [128, 4, 128])
    ckvT = k.sb("mckvT", [128, 2, 128])
    qsb = k.sb("mqsb", [128, 8, 192])
    tq = k.sb("mtq", [128, 1536])
    junk = tq
    s8 = k.sb("ms8", [128, 4, 8])
    kvsb = k.sb("mkvsb", [128, 8, 256])
    kf = k.sb("mkf", [128, 8, 192])
    krg = k.sb("mkrg", [128, 64])
    tb = k.sb("mtb", [128, 4, 256])
    rt = [k.sb("mrt%d" % i, [128, 64]) for i in range(1)]
    Tn = [k.sb("mTn%d" % i, [128, 8, 128]) for i in range(1)]
    Tr = [k.sb("mTr%d" % i, [64, 8, 128]) for i in range(1)]
    Vs = [k.sb("mVs%d" % i, [128, 8, 129]) for i in range(1)]
    for v_ in Vs:
        k.op("pool", lambda e, v_=v_: e.memset(v_[:], 1.0), wr=[v_])
    cnt = [0]

    def head_rstd(src3, dstcol, extra=None):
        n = src3.shape[2]
        tqv = tq[:, :8 * n].rearrange("p (h c) -> p h c", h=8)
        k.op("act", lambda e: e.activation(out=tqv, in_=src3, func=AF.Square), rd=[qsb, kvsb], wr=[tq])
        k.op("dve", lambda e: e.tensor_reduce(out=s8[:, 0, :], in_=tqv, axis=AX.X, op=ALU.add),
             rd=[tq], wr=[s8])
        if extra is not None:
            k.op("dve", lambda e: e.tensor_scalar(out=s8[:, 0, :], in0=s8[:, 0, :], scalar1=extra, scalar2=None,
                                                  op0=ALU.add), rd=[s8, st2], wr=[s8])
        k.op("dve", lambda e: e.tensor_scalar(out=s8[:, 1, :], in0=s8[:, 0, :], scalar1=1.0 / 192, scalar2=EPS,
                                              op0=ALU.mult, op1=ALU.add), rd=[s8], acc=[s8])
        k.op("act", lambda e: e.activation(out=s8[:, 2, :], in_=s8[:, 1, :], func=AF.Sqrt), rd=[s8], acc=[s8])
        k.op("dve", lambda e: e.reciprocal(out=s8[:, 3, :], in_=s8[:, 2, :]), rd=[s8], acc=[s8])

    def emit_T(src, srcbuf, dn, dr, col):
        tn, tr_ = Tn[0], Tr[0]
        for g in range(2):
            tp_group(k, C, [src[:, g * 4 + c, 0:128] for c in range(4)], [srcbuf],
                     tn[:, g * 4:(g + 1) * 4, :], tn, g == 0, eng="act" if g else "dve")
        for g in range(2):
            tp_group(k, C, [src[:, g * 4 + c, 128:192] for c in range(4)], [srcbuf],
                     tr_[:, g * 4:(g + 1) * 4, :], tr_, g == 0, w=64, eng="dve" if g else "act")
        k.store(dn, dn[:, :, col:col + 128].rearrange("h d t -> d h t"), tn, tn[:])
        k.store(dr, dr[:, :, col:col + 128].rearrange("h d t -> d h t"), tr_, tr_[:])

    def tile_A(kind, tile, ms, seq_t0, pb):
        i = 0
        col = tile * 128
        ck = ckv[i]
        pp = psb[i]
        if kind != 'C':
            x = xt[i]
            k.load(x, x[:], xres, xres[col:col + 128, :])
            front(k, x, h, st, junk, C["mods"][ms][0], C["mods"][ms][1])
            to_hT(k, C, h, hT, 0, True)
            for (c0, c1) in ((0, 512), (512, 832)):
                p = nps(C)
                for kc in range(8):
                    k.mm(p, p[:, :c1 - c0], hT[:, kc, :], win[:, kc, c0:c1], kc == 0, kc == 7, rd=[hT, win])
                evac(k, "act", pp[:, c0:c1], p[:, :c1 - c0], rd=[p], wr=[pp] if c0 == 0 else (),
                     acc=() if c0 == 0 else [pp])
            k.op("act", lambda e: e.activation(out=junk[:, :512], in_=pp[:, 0:512], func=AF.Square,
                                               accum_out=st[:, 0:1]), rd=[pp], wr=[junk, st])
            rstd_from_ss(k, st, 512)
            k.op("dve", lambda e: e.scalar_tensor_tensor(out=cq[:], in0=pp[:, 0:512], scalar=st[:, 3:4],
                                                         in1=gains[:, 0:512], op0=ALU.mult, op1=ALU.mult),
                 rd=[pp, st, gains], wr=[cq])
            k.op("act", lambda e: e.activation(out=junk[:, :256], in_=pp[:, 512:768], func=AF.Square,
                                               accum_out=st[:, 0:1]), rd=[pp], wr=[junk, st])
            rstd_from_ss(k, st, 256)
            k.op("dve", lambda e: e.scalar_tensor_tensor(out=ck[:], in0=pp[:, 512:768], scalar=st[:, 3:4],
                                                         in1=gains[:, 512:768], op0=ALU.mult, op1=ALU.mult),
                 rd=[pp, st, gains], wr=[ck])
            if kind == 'P':
                lr = (tile - seq_t0) * 128
                oc, okr = C["out_ckv"], C["out_kr"]
                k.store(oc, oc[pb, j, lr:lr + 128, :], ck, ck[:])
                k.store(okr, okr[pb, j, lr:lr + 128, :], pp, pp[:, 768:832])
            tp_group(k, C, [cq[:, c * 128:(c + 1) * 128] for c in range(4)], [cq], cqT[:], cqT, True)
            for b3 in range(3):
                p = nps(C)
                for kc in range(4):
                    k.mm(p, p[:], cqT[:, kc, :], wuq[:, kc, b3 * 512:(b3 + 1) * 512], kc == 0, kc == 3,
                         rd=[cqT, wuq])
                evac(k, "act" if b3 % 2 else "dve", qsb[:].rearrange("p h c -> p (h c)")[:, b3 * 512:(b3 + 1) * 512],
                     p[:], rd=[p], wr=[qsb] if b3 == 0 else (), acc=() if b3 == 0 else [qsb])
            head_rstd(qsb[:], 0)
            k.op("dve", lambda e: e.tensor_tensor(out=qsb[:], in0=qsb[:], in1=bc(s8[:, 3, :].unsqueeze(2), [128, 8, 192]),
                                                  op=ALU.mult), rd=[qsb, s8], wr=[qsb])
            k.op("pool", lambda e: e.tensor_tensor(out=qsb[:], in0=qsb[:],
                                                   in1=bc(gains[:, 768:960].unsqueeze(1), [128, 8, 192]),
                                                   op=ALU.mult), rd=[qsb, gains], wr=[qsb])
            if kind == 'S':
                r_ = rt[i]
                k.load(r_, r_[:], C["c_rope"], C["c_rope"][col:col + 128, :])
                rope(k, qsb[:, :, 128:192], qsb, r_, tb)
            emit_T(qsb[:], qsb, QTn, QTr, col)
        else:
            lr = (tile - 36) * 128
            k.load(ck, ck[:], C["cache_ckv"], C["cache_ckv"][j, lr:lr + 128, :])
            k.load(pp, pp[:, 768:832], C["cache_kr"], C["cache_kr"][j, lr:lr + 128, :])
        tp_group(k, C, [ck[:, c * 128:(c + 1) * 128] for c in range(2)], [ck], ckvT[:], ckvT, True)
        for b4 in range(4):
            p = nps(C)
            for kc in range(2):
                k.mm(p, p[:], ckvT[:, kc, :], wukv[:, kc, b4 * 512:(b4 + 1) * 512], kc == 0, kc == 1,
                     rd=[ckvT, wukv])
            evac(k, "act" if b4 % 2 else "dve", kvsb[:].rearrange("p h c -> p (h c)")[:, b4 * 512:(b4 + 1) * 512],
                 p[:], rd=[p], wr=[kvsb] if b4 == 0 else (), acc=() if b4 == 0 else [kvsb])
        k.op("act", lambda e: e.activation(out=junk[:, :64], in_=pp[:, 768:832], func=AF.Square,
                                           accum_out=st2[:, 0:1]), rd=[pp], wr=[junk, st2])
        head_rstd(kvsb[:, :, 0:128], 0, extra=st2[:, 0:1])
        k.op("dve", lambda e: e.tensor_tensor(out=kf[:, :, 0:128], in0=kvsb[:, :, 0:128],
                                              in1=bc(s8[:, 3, :].unsqueeze(2), [128, 8, 128]), op=ALU.mult),
             rd=[kvsb, s8], wr=[kf])
        k.op("pool", lambda e: e.tensor_tensor(out=kf[:, :, 0:128], in0=kf[:, :, 0:128],
                                               in1=bc(gains[:, 960:1088].unsqueeze(1), [128, 8, 128]), op=ALU.mult),
             rd=[kf, gains], wr=[kf])
        k.op("dve", lambda e: e.tensor_tensor(out=krg[:], in0=pp[:, 768:832], in1=gains[:, 1088:1152], op=ALU.mult),
             rd=[pp, gains], wr=[krg])
        k.op("dve", lambda e: e.tensor_tensor(out=kf[:, :, 128:192], in0=bc(krg[:].unsqueeze(1), [128, 8, 64]),
                                              in1=bc(s8[:, 3, :].unsqueeze(2), [128, 8, 64]), op=ALU.mult),
             rd=[krg, s8, kf], wr=[kf])
        if kind == 'S':
            rope(k, kf[:, :, 128:192], kf, rt[i], tb)
        v_ = Vs[0]
        k.op("pool", lambda e: e.tensor_copy(out=v_[:, :, 0:128], in_=kvsb[:, :, 128:256]), rd=[kvsb], wr=[v_])
        k.store(Vp, Vp[:, col:col + 128, :].rearrange("h t c -> t h c"), v_, v_[:])
        emit_T(kf[:], kf, KTn, KTr, col)

    for (t0, ntl, ms, pb) in seqs:
        for tile in range(t0, t0 + ntl):
            tile_A('S' if ms == 0 else 'P', tile, ms, t0, pb)
    for tile in (36, 37):
        tile_A('C', tile, 0, 36, 0)
    k.phase_end()
    k.phase_begin()
    Kn = k.sb("aKn", [128, 4352])
    Kr = k.sb("aKr", [64, 4352])
    Vh = k.sb("aVh", [128, 34, 129])
    Qn = k.sb("aQn", [128, 4096])
    Qr = k.sb("aQr", [64, 4096])
    PT = [k.sb("aPT%d" % i, [128, 512]) for i in range(3)]
    Os = [k.sb("aOs%d" % i, [128, 128]) for i in range(2)]
    rs = [k.sb("ars%d" % i, [128, 1]) for i in range(2)]
    oi = 0
    for (t0, ntl, ms, pb) in seqs:
        Lq = ntl * 128
        q0 = t0 * 128
        runs = [(q0, Lq)] + ([(4608, 256)] if ms == 0 else [])
        nk = sum(r[1] for r in runs) // 128
        for hh in range(8):
            o = 0
            for ri, (r0, rl) in enumerate(runs):
                k.load(Kn, Kn[:, o:o + rl], KTn, KTn[hh, :, r0:r0 + rl], acc=(ri > 0))
                k.load(Kr, Kr[:, o:o + rl], KTr, KTr[hh, :, r0:r0 + rl], acc=(ri > 0))
                k.load(Vh, Vh[:, o // 128:(o + rl) // 128, :], Vp,
                       Vp[hh, r0:r0 + rl, :].rearrange("(n p) c -> p n c", p=128), acc=(ri > 0))
                o += rl
            k.load(Qn, Qn[:, :Lq], QTn, QTn[hh, :, q0:q0 + Lq])
            k.load(Qr, Qr[:, :Lq], QTr, QTr[hh, :, q0:q0 + Lq])
            for qg in range(0, Lq, 512):
                ncol = min(512, Lq - qg)
                nj = ncol // 128
                for ki in range(nk):
                    sp_ = C["ps"][4 + ki % 4]
                    k.mm(sp_, sp_[:, :ncol], Kn[:, ki * 128:(ki + 1) * 128], Qn[:, qg:qg + ncol], True, False,
                         rd=[Kn, Qn])
                    k.mm(sp_, sp_[:, :ncol], Kr[:, ki * 128:(ki + 1) * 128], Qr[:, qg:qg + ncol], False, True,
                         rd=[Kr, Qr])
                    pt = PT[ki % 3]
                    k.op("act", lambda e, pt=pt, sp_=sp_, ncol=ncol: e.activation(
                        out=pt[:, :ncol], in_=sp_[:, :ncol], func=AF.Exp), rd=[sp_], wr=[pt])
                    for jj in range(nj):
                        po = C["ps"][jj]
                        k.mm(po, po[:, 0:129], pt[:, jj * 128:(jj + 1) * 128], Vh[:, ki, :], ki == 0, ki == nk - 1,
                             rd=[pt, Vh])
                for jj in range(nj):
                    po = C["ps"][jj]
                    os_, r_ = Os[oi % 2], rs[oi % 2]
                    oi += 1
                    k.op("dve", lambda e, po=po, r_=r_: e.reciprocal(out=r_[:], in_=po[:, 128:129]), rd=[po], wr=[r_])
                    k.op("dve", lambda e, po=po, r_=r_, os_=os_: e.tensor_scalar(
                        out=os_[:], in0=po[:, 0:128], scalar1=r_[:, 0:1], scalar2=None, op0=ALU.mult),
                        rd=[po, r_], wr=[os_])
                    tok = q0 + qg + jj * 128
                    k.store(Od, Od[tok:tok + 128, hh * 128:(hh + 1) * 128], os_, os_[:])
    k.phase_end()
    outproj_phase(k, C, xres, C["odd_w_out"][j], C["odd_w_out"], Od, [(s[0], s[1], s[2]) for s in seqs])


BIG = 30000.0


def even_phase(k, C, l, xres, seqs):
    j = l // 2
    PTd, Gd, BGd, Zd = C["PT_d"], C["G_d"], C["BG_d"], C["Z_d"]
    QTd, KTd, Kd, Vd, Mix = C["gQT"], C["gKT"], C["gK"], C["gV"], C["Mix_d"]
    ones = C["ones"]
    k.phase_begin()
    win = k.sb("ewin", [128, 8, 2576])
    k.load(win, win[:], C["even_w_in"], C["even_w_in"][j].rearrange("(c p) n -> p c n", p=128))
    rows = k.sb("erows", [1, 16])
    k.load(rows, rows[0:1, 0:8], C["gdn_a_log"], C["gdn_a_log"][j:j + 1].rearrange("a d h -> a (d h)"))
    k.load(rows, rows[0:1, 8:16], C["gdn_dt_bias"], C["gdn_dt_bias"][j:j + 1].rearrange("a d h -> a (d h)"), acc=True)
    adt = k.sb("eadt", [128, 16])
    rep_rows(k, C, rows, 0, 16, adt)
    k.op("act", lambda e: e.activation(out=adt[:, 0:8], in_=adt[:, 0:8], func=AF.Exp), rd=[adt], wr=[adt])
    cs128 = k.sb("ecs128", [128, 256])
    k.load(cs128, cs128[:], C["c_cs128"], C["c_cs128"][:, :])
    x = k.sb("ex", [128, D])
    h = k.sb("eh", [128, D])
    junk = k.sb("ejunk", [128, D])
    st = k.sb("est", [128, 4])
    hT = k.sb("ehT", [128, 8, 512])
    gsb = k.sb("egsb", [128, 512])
    bg = k.sb("ebg", [128, 16])
    zt = k.sb("ezt", [128, 8])
    ptb = [k.sb("eptb%d" % i, [128, 512]) for i in range(2)]
    xbT = [k.sb("exbT%d" % i, [128, 512]) for i in range(4)]
    zsb = k.sb("ezsb", [128, 2, 4, 128])
    for (t0, ntl, ms, pb) in seqs:
        a1, sh1 = C["mods"][ms][0], C["mods"][ms][1]
        for gs in range(t0, t0 + ntl, 4):
            gn = min(4, t0 + ntl - gs)
            ncol = gn * 128
            for jj in range(gn):
                tile = gs + jj
                rws = slice(tile * 128, (tile + 1) * 128)
                k.load(x, x[:], xres, xres[rws, :])
                front(k, x, h, st, junk, a1, sh1)
                to_hT(k, C, h, hT, jj * 128, jj == 0)
                p = nps(C)
                for kc in range(8):
                    k.mm(p, p[:], hT[:, kc, jj * 128:(jj + 1) * 128], win[:, kc, 1536:2048], kc == 0, kc == 7, rd=[hT, win])
                k.op("act", lambda e, p=p: e.activation(out=gsb[:], in_=p[:], func=AF.Silu), rd=[p], wr=[gsb])
                k.store(Gd, Gd[rws, :], gsb, gsb[:])
                p2 = nps(C)
                for kc in range(8):
                    k.mm(p2, p2[:, 0:16], hT[:, kc, jj * 128:(jj + 1) * 128], win[:, kc, 2048:2064], kc == 0, kc == 7,
                         rd=[hT, win])
                k.op("act", lambda e, p2=p2: e.activation(out=bg[:, 0:8], in_=p2[:, 0:8], func=AF.Sigmoid), rd=[p2], wr=[bg])
                k.op("dve", lambda e, p2=p2: e.tensor_tensor(out=zt[:], in0=p2[:, 8:16], in1=adt[:, 8:16], op=ALU.add),
                     rd=[p2, adt], wr=[zt])
                k.op("act", lambda e: e.activation(out=zt[:], in_=zt[:], func=AF.Exp), rd=[zt], wr=[zt])
                k.op("act", lambda e: e.activation(out=zt[:], in_=zt[:], func=AF.Ln, bias=1.0), rd=[zt], wr=[zt])
                k.op("dve", lambda e: e.scalar_tensor_tensor(out=bg[:, 8:16], in0=zt[:], scalar=-1.0, in1=adt[:, 0:8],
                                                             op0=ALU.mult, op1=ALU.mult), rd=[zt, adt], acc=[bg])
                k.store(BGd, BGd[rws, :], bg, bg[:])
            tok0 = gs * 128
            for c in range(12):
                p = nps(C)
                for kc in range(8):
                    k.mm(p, p[:, :ncol], win[:, kc, c * 128:(c + 1) * 128], hT[:, kc, :ncol], kc == 0, kc == 7, rd=[hT, win])
                pt_ = ptb[c % 2]
                evac(k, "act" if c % 2 else "dve", pt_[:, :ncol], p[:, :ncol], rd=[p], wr=[pt_])
                k.store(PTd, PTd[c, :, tok0:tok0 + ncol], pt_, pt_[:, :ncol])
            for g4 in range(4):
                p = nps(C)
                for kc in range(8):
                    k.mm(p, p[:, :ncol], win[:, kc, 2576 - 512 + g4 * 128:2576 - 512 + (g4 + 1) * 128], hT[:, kc, :ncol],
                         kc == 0, kc == 7, rd=[hT, win])
                evac(k, "act" if g4 % 2 else "dve", xbT[g4][:, :ncol], p[:, :ncol], rd=[p], wr=[xbT[g4]])
            for jj in range(gn):
                tile = gs + jj
                for g4 in range(4):
                    p = nps(C)
                    k.mm(p, p[:, 0:256], xbT[g4][:, jj * 128:(jj + 1) * 128], cs128[:], True, True, rd=[xbT[g4], cs128])
                    evac(k, "act" if g4 % 2 else "dve", zsb[:, :, g4, :],
                         p[:, 0:256].rearrange("p (c m) -> p c m", c=2), rd=[p],
                         wr=[zsb] if g4 == 0 else (), acc=() if g4 == 0 else [zsb])
                k.store(Zd, Zd[tile * 128:(tile + 1) * 128, :], zsb, zsb[:].rearrange("p c g m -> p (c g m)"))
    k.phase_end()
    k.phase_begin()
    cw = k.sb("ecw", [128, 3, 12])
    for wi in range(3):
        k.load(cw, cw[:, wi, :], C["even_conv_w"], C["even_conv_w"][j, wi, :].rearrange("(c p) -> p c", p=128),
               acc=(wi > 0), allow_slow_non_contiguous=True)
    LM = 4096
    raw = k.sb("eraw", [128, LM + 2])
    y = k.sb("ey", [128, LM])
    sq = k.sb("esq", [128, LM])
    rs_ = k.sb("ers", [128, 512])
    tkm = [k.sb("etk%d" % i, [128, 4, 128]) for i in range(2)]
    ci_ = 0
    for (t0, ntl, ms, pb) in seqs:
        L = ntl * 128
        tok0 = t0 * 128
        for c in range(12):
            k.op("pool", lambda e: e.memset(raw[:, 0:1], 0.0), wr=[raw])
            k.op("pool", lambda e, L=L: e.memset(raw[:, L + 1:L + 2], 0.0), acc=[raw])
            k.load(raw, raw[:, 1:L + 1], PTd, PTd[c, :, tok0:tok0 + L], acc=True)
            k.op("dve", lambda e, L=L, c=c: e.tensor_scalar(out=y[:, :L], in0=raw[:, 0:L], scalar1=cw[:, 0, c:c + 1],
                                                            scalar2=None, op0=ALU.mult), rd=[raw, cw], wr=[y])
            for wi in (1, 2):
                k.op("dve", lambda e, L=L, c=c, wi=wi: e.scalar_tensor_tensor(
                    out=y[:, :L], in0=raw[:, wi:wi + L], scalar=cw[:, wi, c:c + 1], in1=y[:, :L], op0=ALU.mult,
                    op1=ALU.add), rd=[raw, cw, y], wr=[y])
            k.op("act", lambda e, L=L: e.activation(out=y[:, :L], in_=y[:, :L], func=AF.Silu), rd=[y], wr=[y])
            if c < 8:
                k.op("act", lambda e, L=L: e.activation(out=sq[:, :L], in_=y[:, :L], func=AF.Square), rd=[y], wr=[sq])
                for b0 in range(0, L, 512):
                    w = min(512, L - b0)
                    p = nps(C)
                    k.mm(p, p[:, :w], ones[:], sq[:, b0:b0 + w], True, True, rd=[ones, sq])
                    k.op("dve", lambda e, p=p, w=w: e.tensor_scalar(out=rs_[:, :w], in0=p[:, :w], scalar1=EPS, scalar2=None,
                                                                    op0=ALU.add), rd=[p], wr=[rs_])
                    k.op("act", lambda e, w=w: e.activation(out=rs_[:, :w], in_=rs_[:, :w], func=AF.Sqrt), rd=[rs_], wr=[rs_])
                    k.op("dve", lambda e, w=w: e.reciprocal(out=rs_[:, :w], in_=rs_[:, :w]), rd=[rs_], wr=[rs_])
                    if c < 4:
                        k.op("dve", lambda e, w=w, b0=b0: e.scalar_tensor_tensor(
                            out=y[:, b0:b0 + w], in0=y[:, b0:b0 + w], scalar=128.0 ** -0.5, in1=rs_[:, :w],
                            op0=ALU.mult, op1=ALU.mult), rd=[y, rs_], wr=[y])
                    else:
                        k.op("dve", lambda e, w=w, b0=b0: e.tensor_tensor(
                            out=y[:, b0:b0 + w], in0=y[:, b0:b0 + w], in1=rs_[:, :w], op=ALU.mult), rd=[y, rs_], wr=[y])
                dst = QTd if c < 4 else KTd
                k.store(dst, dst[c % 4, :, tok0:tok0 + L], y, y[:, :L])
            if c >= 4:
                dst = Kd if c < 8 else Vd
                for g in range(0, ntl, 4):
                    gn = min(4, ntl - g)
                    tk = tkm[ci_ % 2]
                    ci_ += 1
                    tp_group(k, C, [y[:, (g + q) * 128:(g + q + 1) * 128] for q in range(gn)], [y],
                             tk[:, :gn, :], tk, True, eng="act" if ci_ % 2 else "dve")
                    r0 = tok0 + g * 128
                    k.store(dst, dst[r0:r0 + gn * 128, c % 4, :].rearrange("(n p) d -> p n d", p=128), tk, tk[:, :gn, :])
    k.phase_end()
    k.phase_begin()
    G = C["c_gdn"]
    cg = k.sb("gconst", [128, 6, 128])
    k.load(cg, cg[:], G, G[:, :, :].rearrange("a p m -> p a m"))
    negones = k.sb("gnegones", [128, 128])
    k.op("pool", lambda e: e.memset(negones[:], -1.0), wr=[negones])
    ident = C["ident"]
    NTm = 32
    QT = k.sb("gsQT", [128, NTm * 128])
    KT = k.sb("gsKT", [128, NTm * 128])
    Kt = k.sb("gKt", [128, NTm, 128])
    Vt = k.sb("gVt", [128, NTm, 128])
    BG = k.sb("gBG", [128, NTm, 16])
    Oa = k.sb("gOa", [128, NTm, 128])
    S = k.sb("gS", [128, 128])
    GT = k.sb("gGT", [128, 128])
    gcs = k.sb("ggcs", [128, 8])
    Dm = k.sb("gDm", [128, 128])
    DT = k.sb("gDT", [128, 128])
    t1 = k.sb("gt1", [128, 128])
    Nn = [k.sb("gN%d" % i, [128, 128]) for i in range(2)]
    NnT = [k.sb("gNT%d" % i, [128, 128]) for i in range(2)]
    Pp = [k.sb("gP%d" % i, [128, 128]) for i in range(2)]
    QKD = k.sb("gQKD", [128, 128])
    ru = k.sb("gru", [128, 128])
    rw = k.sb("grw", [128, 128])
    nwT = k.sb("gnwT", [128, 128])
    vn = k.sb("gvn", [128, 128])
    o2s = k.sb("go2s", [128, 128])
    kd = k.sb("gkd", [128, 128])
    ot = k.sb("got", [128, 128])

    def sq_mm(lhsT, rhs, rd):
        p = nps(C)
        k.mm(p, p[:, :128], lhsT, rhs, True, True, rd=rd)
        return p

    for (t0, ntl, ms, pb) in seqs:
        L = ntl * 128
        tok0 = t0 * 128
        for hh in range(4):
            k.load(QT, QT[:, :L], QTd, QTd[hh, :, tok0:tok0 + L])
            k.load(KT, KT[:, :L], KTd, KTd[hh, :, tok0:tok0 + L])
            k.load(Kt, Kt[:, :ntl, :], Kd, Kd[tok0:tok0 + L, hh, :].rearrange("(n p) d -> p n d", p=128))
            k.load(Vt, Vt[:, :ntl, :], Vd, Vd[tok0:tok0 + L, hh, :].rearrange("(n p) d -> p n d", p=128))
            k.load(BG, BG[:, :ntl, :], BGd, BGd[tok0:tok0 + L, :].rearrange("(n p) d -> p n d", p=128))
            for dr in range(2):
                Tri, Mk, St = cg[:, dr, :], cg[:, 2 + dr, :], cg[:, 4 + dr, :]
                if ms == 0:
                    k.load(S, S[:], C["state_gdn"], C["state_gdn"][j, dr, hh, :, :])
                else:
                    k.op("pool", lambda e: e.memset(S[:], 0.0), wr=[S])
                order = range(ntl) if dr == 0 else range(ntl - 1, -1, -1)
                for n in order:
                    cs_ = slice(n * 128, (n + 1) * 128)
                    beta = BG[:, n, dr * 4 + hh:dr * 4 + hh + 1]
                    gg = BG[:, n, 8 + dr * 4 + hh:8 + dr * 4 + hh + 1]
                    k.op("dve", lambda e, Tri=Tri, gg=gg: e.tensor_scalar(out=GT[:], in0=Tri, scalar1=gg, scalar2=None,
                                                                          op0=ALU.mult), rd=[cg, BG], wr=[GT])
                    pg = nps(C)
                    k.op("pe", lambda e, pg=pg: e.matmul(pg[:, 0:1], GT[:], ones[:, 0:1], start=True, stop=True),
                         rd=[GT, ones], wr=[pg])
                    pg2 = nps(C)
                    k.op("pe", lambda e, pg2=pg2, gg=gg: e.matmul(pg2[:, 0:1], ones[:], gg, start=True, stop=True),
                         rd=[BG, ones], wr=[pg2])
                    k.op("dve", lambda e, pg=pg: e.tensor_copy(out=gcs[:, 0:1], in_=pg[:, 0:1]), rd=[pg], wr=[gcs])
                    k.op("dve", lambda e, pg2=pg2: e.tensor_copy(out=gcs[:, 1:2], in_=pg2[:, 0:1]), rd=[pg2], acc=[gcs])
                    k.op("act", lambda e: e.activation(out=gcs[:, 2:3], in_=gcs[:, 0:1], func=AF.Exp), rd=[gcs], acc=[gcs])
                    k.op("act", lambda e: e.activation(out=gcs[:, 3:4], in_=gcs[:, 0:1], func=AF.Exp, scale=-1.0,
                                                       bias=gcs[:, 1:2]), rd=[gcs], acc=[gcs])
                    k.op("act", lambda e: e.activation(out=gcs[:, 4:5], in_=gcs[:, 1:2], func=AF.Exp), rd=[gcs], acc=[gcs])
                    k.op("dve", lambda e, beta=beta: e.tensor_scalar(out=gcs[:, 5:6], in0=beta, scalar1=-1.0, scalar2=None,
                                                                     op0=ALU.mult), rd=[BG], acc=[gcs])
                    k.op("dve", lambda e, beta=beta: e.tensor_tensor(out=gcs[:, 6:7], in0=beta, in1=gcs[:, 2:3], op=ALU.mult),
                         rd=[BG, gcs], acc=[gcs])
                    pH = nps(C)
                    k.mm(pH, pH[:, :128], GT[:], ones[:], True, False, rd=[GT, ones])
                    k.mm(pH, pH[:, :128], negones[:], GT[:], False, True, rd=[GT, negones])
                    k.op("dve", lambda e, pH=pH, Mk=Mk: e.tensor_tensor(out=Dm[:], in0=pH[:, :128], in1=Mk, op=ALU.add),
                         rd=[pH, cg], wr=[Dm])
                    k.op("act", lambda e: e.activation(out=Dm[:], in_=Dm[:], func=AF.Exp), rd=[Dm], wr=[Dm])
                    pKK = sq_mm(KT[:, cs_], KT[:, cs_], [KT])
                    pQK = sq_mm(KT[:, cs_], QT[:, cs_], [KT, QT])
                    k.op("dve", lambda e, pKK=pKK: e.tensor_tensor(out=t1[:], in0=pKK[:, :128], in1=Dm[:], op=ALU.mult),
                         rd=[pKK, Dm], wr=[t1])
                    N0, NT0 = Nn[0], NnT[0]
                    k.op("dve", lambda e, St=St, N0=N0: e.scalar_tensor_tensor(out=N0[:], in0=t1[:], scalar=gcs[:, 5:6], in1=St,
                                                                               op0=ALU.mult, op1=ALU.mult),
                         rd=[t1, gcs, cg], wr=[N0])
                    pT = nps(C)
                    k.tr(pT, pT[:, :128], N0[:], ident[:], rd=[N0, ident])
                    evac(k, "act", NT0[:], pT[:, :128], rd=[pT], wr=[NT0])
                    pT2 = nps(C)
                    k.tr(pT2, pT2[:, :128], Dm[:], ident[:], rd=[Dm, ident])
                    evac(k, "act", DT[:], pT2[:, :128], rd=[pT2], wr=[DT])
                    k.op("dve", lambda e, pQK=pQK: e.tensor_tensor(out=QKD[:], in0=pQK[:, :128], in1=DT[:], op=ALU.mult),
                         rd=[pQK, DT], wr=[QKD])
                    P0 = Pp[0]
                    k.op("pool", lambda e, P0=P0, NT0=NT0: e.tensor_tensor(out=P0[:], in0=NT0[:], in1=ident[:], op=ALU.add),
                         rd=[NT0, ident], wr=[P0])
                    cur = 0
                    for lvl in range(1, 7):
                        Nc, NTc, Nx, NTx = Nn[cur], NnT[cur], Nn[1 - cur], NnT[1 - cur]
                        Pc, Px = Pp[cur], Pp[1 - cur]
                        p1 = sq_mm(NTc[:], Nc[:], [NTc, Nc])
                        if lvl < 6:
                            p2 = sq_mm(Nc[:], NTc[:], [NTc, Nc])
                        evac(k, "act", Nx[:], p1[:, :128], rd=[p1], wr=[Nx])
                        if lvl < 6:
                            evac(k, "dve", NTx[:], p2[:, :128], rd=[p2], wr=[NTx])
                        p3 = sq_mm(Nx[:], Pc[:], [Nx, Pc])
                        k.op("dve", lambda e, p3=p3, Pc=Pc, Px=Px: e.tensor_tensor(out=Px[:], in0=p3[:, :128], in1=Pc[:],
                                                                                   op=ALU.add), rd=[p3, Pc], wr=[Px])
                        cur = 1 - cur
                    PTt = Pp[cur]
                    k.op("pool", lambda e, n=n, beta=beta: e.tensor_scalar(out=ru[:], in0=Vt[:, n, :], scalar1=beta, scalar2=None,
                                                                           op0=ALU.mult), rd=[Vt, BG], wr=[ru])
                    k.op("dve", lambda e, n=n: e.tensor_scalar(out=rw[:], in0=Kt[:, n, :], scalar1=gcs[:, 6:7], scalar2=None,
                                                               op0=ALU.mult), rd=[Kt, gcs], wr=[rw])
                    pw = sq_mm(rw[:], PTt[:], [rw, PTt])
                    evac(k, "act", nwT[:], pw[:, :128], rd=[pw], wr=[nwT], scale=-1.0)
                    pv = nps(C)
                    k.mm(pv, pv[:, :128], PTt[:], ru[:], True, False, rd=[PTt, ru])
                    k.mm(pv, pv[:, :128], nwT[:], S[:], False, True, rd=[nwT, S])
                    evac(k, "act", vn[:], pv[:, :128], rd=[pv], wr=[vn])
                    po1 = sq_mm(QT[:, cs_], S[:], [QT, S])
                    po2 = sq_mm(QKD[:], vn[:], [QKD, vn])
                    evac(k, "act", o2s[:], po2[:, :128], rd=[po2], wr=[o2s])
                    if dr == 0:
                        k.op("dve", lambda e, po1=po1, n=n: e.scalar_tensor_tensor(
                            out=Oa[:, n, :], in0=po1[:, :128], scalar=gcs[:, 2:3], in1=o2s[:], op0=ALU.mult, op1=ALU.add),
                            rd=[po1, gcs, o2s], wr=[Oa] if n == 0 else (), acc=() if n == 0 else [Oa])
                    else:
                        k.op("dve", lambda e, po1=po1: e.scalar_tensor_tensor(
                            out=ot[:], in0=po1[:, :128], scalar=gcs[:, 2:3], in1=o2s[:], op0=ALU.mult, op1=ALU.add),
                            rd=[po1, gcs, o2s], wr=[ot])
                        k.op("pool", lambda e, n=n: e.tensor_tensor(out=Oa[:, n, :], in0=Oa[:, n, :], in1=ot[:], op=ALU.add),
                             rd=[ot, Oa], acc=[Oa])
                    k.op("pool", lambda e, n=n: e.tensor_scalar(out=kd[:], in0=Kt[:, n, :], scalar1=gcs[:, 3:4], scalar2=None,
                                                                op0=ALU.mult), rd=[Kt, gcs], wr=[kd])
                    pS = sq_mm(kd[:], vn[:], [kd, vn])
                    k.op("dve", lambda e, pS=pS: e.scalar_tensor_tensor(out=S[:], in0=S[:], scalar=gcs[:, 4:5], in1=pS[:, :128],
                                                                        op0=ALU.mult, op1=ALU.add), rd=[S, gcs, pS], wr=[S])
                if ms == 1:
                    og = C["out_gdn"]
                    k.store(og, og[pb, j, dr, hh, :, :], S, S[:])
            k.store(Mix, Mix[tok0:tok0 + L, hh * 128:(hh + 1) * 128].rearrange("(n p) d -> p n d", p=128), Oa, Oa[:, :ntl, :])
    k.phase_end()
    k.phase_begin()
    Zc = k.sb("fZc", [128, 32, 256])
    Zs = k.sb("fZs", [128, 32, 256])
    tC = [k.sb("ftC%d" % i, [128, 32, 128]) for i in range(2)]
    tS = [k.sb("ftS%d" % i, [128, 32, 128]) for i in range(2)]
    fo = [k.sb("ffo%d" % i, [128, 256]) for i in range(2)]
    it = 0
    for (t0, ntl, ms, pb) in seqs:
        L = ntl * 128
        tok0 = t0 * 128
        tab = C["c_dftS"] if ms == 0 else C["c_dftP"]
        for gp in range(2):
            zv = Zd[tok0:tok0 + L, :].rearrange("(n p) (c g m) -> p n c g m", p=128, c=2, g=4)
            for q2 in range(2):
                k.load(Zc, Zc[:, :ntl, q2 * 128:(q2 + 1) * 128], Zd, zv[:, :, 0, gp * 2 + q2, :], acc=(q2 > 0))
                k.load(Zs, Zs[:, :ntl, q2 * 128:(q2 + 1) * 128], Zd, zv[:, :, 1, gp * 2 + q2, :], acc=(q2 > 0))
            for lt in range(ntl):
                a, b = tC[it % 2], tS[it % 2]
                f = fo[it % 2]
                it += 1
                k.load(a, a[:, :ntl, :], tab, tab[0, :, lt * 128:(lt + 1) * 128].rearrange("(n p) m -> p n m", p=128))
                k.load(b, b[:, :ntl, :], tab, tab[1, :, lt * 128:(lt + 1) * 128].rearrange("(n p) m -> p n m", p=128))
                p = nps(C)
                for n in range(ntl):
                    k.mm(p, p[:, :256], a[:, n, :], Zc[:, n, :], n == 0, False, rd=[a, Zc])
                    k.mm(p, p[:, :256], b[:, n, :], Zs[:, n, :], False, n == ntl - 1, rd=[b, Zs])
                evac(k, "act", f[:], p[:, :256], rd=[p], wr=[f])
                r0 = tok0 + lt * 128
                k.store(Mix, Mix[r0:r0 + 128, 512 + gp * 256:512 + (gp + 1) * 256], f, f[:])
    k.phase_end()

    def pre_alloc(k):
        a = {}
        a["g"] = k.sb("dg", [128, 512])
        a["sq"] = k.sb("dsq", [128, 512])
        a["s4"] = k.sb("ds4", [128, 4, 4])
        a["rows"] = k.sb("drows", [1, 128])
        k.load(a["rows"], a["rows"][:], C["gdn_o_norm"], C["gdn_o_norm"][j:j + 1, :])
        a["onw"] = k.sb("donw", [128, 128])
        rep_rows(k, C, a["rows"], 0, 128, a["onw"])
        return a

    def pre(k, a, tile, m):
        g, sq, s4, onw = a["g"], a["sq"], a["s4"], a["onw"]
        k.load(g, g[:], Gd, Gd[tile * 128:(tile + 1) * 128, :])
        mv = m[:, 0:512].rearrange("p (h d) -> p h d", h=4)
        k.op("act", lambda e: e.activation(out=sq[:], in_=m[:, 0:512], func=AF.Square), rd=[m], wr=[sq])
        k.op("dve", lambda e: e.tensor_reduce(out=s4[:, 0, :], in_=sq[:].rearrange("p (h d) -> p h d", h=4), axis=AX.X,
                                              op=ALU.add), rd=[sq], wr=[s4])
        k.op("dve", lambda e: e.tensor_scalar(out=s4[:, 1, :], in0=s4[:, 0, :], scalar1=1.0 / 128, scalar2=EPS,
                                              op0=ALU.mult, op1=ALU.add), rd=[s4], acc=[s4])
        k.op("act", lambda e: e.activation(out=s4[:, 2, :], in_=s4[:, 1, :], func=AF.Sqrt), rd=[s4], acc=[s4])
        k.op("dve", lambda e: e.reciprocal(out=s4[:, 3, :], in_=s4[:, 2, :]), rd=[s4], acc=[s4])
        k.op("dve", lambda e: e.tensor_tensor(out=mv, in0=mv, in1=bc(s4[:, 3, :].unsqueeze(2), [128, 4, 128]), op=ALU.mult),
             rd=[m, s4], wr=[m])
        k.op("dve", lambda e: e.tensor_tensor(out=mv, in0=mv, in1=bc(onw[:].unsqueeze(1), [128, 4, 128]), op=ALU.mult),
             rd=[m, onw], wr=[m])
        k.op("pool", lambda e: e.tensor_tensor(out=m[:, 0:512], in0=m[:, 0:512], in1=g[:], op=ALU.mult), rd=[m, g], wr=[m])

    outproj_phase(k, C, xres, C["even_w_out"][j], C["even_w_out"], Mix, [(s[0], s[1], s[2]) for s in seqs],
                  pre=pre, pre_alloc=pre_alloc)


def even_const_shapes():
    return {"c_cs128": (128, 256), "c_gdn": (6, 128, 128), "c_dftS": (2, 4096, 4096), "c_dftP": (2, 256, 256)}


def _dft(L):
    i = np.arange(L, dtype=np.int64)
    ph = (np.outer(i, i) % L).astype(np.float64) * (2.0 * np.pi / L)
    s = 1.0 / np.sqrt(L * 128.0)
    return np.stack([np.cos(ph) * s, -np.sin(ph) * s]).astype(np.float32)


def even_consts():
    i = np.arange(128)
    ph = (np.outer(i, i) % 128).astype(np.float64) * (2.0 * np.pi / 128)
    cs = np.concatenate([np.cos(ph), np.sin(ph)], 1).astype(np.float32)
    kk, ii = np.meshgrid(i, i, indexing="ij")
    triF = (kk <= ii).astype(np.float32)
    triB = (kk >= ii).astype(np.float32)
    mF = np.where(ii.T >= kk.T, 0.0, 0.0)
    r, c = np.meshgrid(i, i, indexing="ij")
    maskF = np.where(c <= r, 0.0, -BIG).astype(np.float32)
    maskB = np.where(c >= r, 0.0, -BIG).astype(np.float32)
    strictF = (c < r).astype(np.float32)
    strictB = (c > r).astype(np.float32)
    g = np.stack([triF, triB, maskF, maskB, strictF, strictB]).astype(np.float32)
    return {"c_cs128": cs, "c_gdn": g, "c_dftS": _dft(4096), "c_dftP": _dft(256)}


def even_scratch(k, C, T):
    C["PT_d"] = k.dram("PT_d", [12, 128, T])
    C["G_d"] = k.dram("G_d", [T, 512])
    C["BG_d"] = k.dram("BG_d", [T, 16])
    C["Z_d"] = k.dram("Z_d", [T, 1024])
    C["gQT"] = k.dram("gQT", [4, 128, T])
    C["gKT"] = k.dram("gKT", [4, 128, T])
    C["gK"] = k.dram("gK", [T, 4, 128])
    C["gV"] = k.dram("gV", [T, 4, 128])
    C["Mix_d"] = k.dram("Mix_d", [T, 1024])


WNAMES = ["w_mod", "b_mod", "norm_mix", "norm_ffn", "even_w_in", "even_conv_w", "gdn_a_log", "gdn_dt_bias",
          "gdn_o_norm", "even_w_out", "odd_w_in", "mla_q_norm", "mla_kv_norm", "mla_w_uq", "mla_w_ukv",
          "mla_q_headnorm", "mla_k_headnorm", "odd_w_out", "peer_w_q", "peer_sub_keys", "peer_u", "peer_v"]
WSHAPES = {"w_mod": (4, 1024, 6144), "b_mod": (4, 6144), "norm_mix": (4, 1024), "norm_ffn": (4, 1024),
           "even_w_in": (2, 1024, 2576), "even_conv_w": (2, 3, 1536), "gdn_a_log": (2, 2, 4), "gdn_dt_bias": (2, 2, 4),
           "gdn_o_norm": (2, 128), "even_w_out": (2, 1024, 1024), "odd_w_in": (2, 1024, 832), "mla_q_norm": (2, 512),
           "mla_kv_norm": (2, 256), "mla_w_uq": (2, 512, 1536), "mla_w_ukv": (2, 256, 2048), "mla_q_headnorm": (2, 192),
           "mla_k_headnorm": (2, 192), "odd_w_out": (2, 1024, 1024), "peer_w_q": (4, 1024, 2048),
           "peer_sub_keys": (4, 8, 2, 128, 128), "peer_u": (4, 16384, 1024), "peer_v": (4, 16384, 1024)}
T_CORE = 4608
SEQS = [(0, 32, 0, 0), (32, 2, 1, 0), (34, 2, 1, 1)]


def rope_table():
    n = 16
    inv = np.power(10000.0, -np.arange(n, dtype=np.float32) / n).astype(np.float32)
    pos = np.arange(4096)
    row = (pos // 64).astype(np.float32)[:, None] * inv
    col = (pos % 64).astype(np.float32)[:, None] * inv
    t = np.zeros((4096, 2, 2, 16), np.float32)
    t[:, 0, 0] = np.cos(row)
    t[:, 0, 1] = np.cos(col)
    t[:, 1, 0] = np.sin(row)
    t[:, 1, 1] = np.sin(col)
    return t.reshape(4096, 64)


def build_program(depth=4, peer=True, mixer=True):
    nc = bass.Bass("TRN2", target_bir_lowering=False)
    with ExitStack() as es:
        k = K(nc, es)
        C = {}
        shapes = dict(WSHAPES)
        shapes.update({"cvec": (2, 1024), "state_gdn": (2, 2, 4, 128, 128), "cache_ckv": (2, 256, 256),
                       "cache_kr": (2, 256, 64), "c_rope": (4096, 64)})
        shapes.update(even_const_shapes())
        for nm, shp in shapes.items():
            C[nm] = k.dram(nm, list(shp), kind="ExternalInput")
        cid = k.dram("c_ident", [128, 128], kind="ExternalInput")
        cio = k.dram("c_iota16", [128, 16], kind="ExternalInput")
        xin = k.dram("x_all", [T_CORE, D], kind="ExternalInput")
        y = k.dram("y_all", [T_CORE, D], kind="ExternalOutput")
        C["out_gdn"] = k.dram("out_gdn", [2, 2, 2, 4, 128, 128], kind="ExternalOutput")
        C["out_ckv"] = k.dram("out_ckv", [2, 2, 256, 256], kind="ExternalOutput")
        C["out_kr"] = k.dram("out_kr", [2, 2, 256, 64], kind="ExternalOutput")
        C["H_dram"] = k.dram("H_dram", [T_CORE, D])
        C["S_dram"] = k.dram("S_dram", [T_CORE, 2048])
        C["QTn"] = k.dram("QTn", [8, 128, 4608])
        C["QTr"] = k.dram("QTr", [8, 64, 4608])
        C["KTn"] = k.dram("KTn", [8, 128, 4864])
        C["KTr"] = k.dram("KTr", [8, 64, 4864])
        C["Vp"] = k.dram("Vp", [8, 4864, 129])
        C["O_d"] = k.dram("O_d", [T_CORE, D])
        even_scratch(k, C, T_CORE)
        C["ps"] = [k.ps("ps%d" % i, [128, 512]) for i in range(8)]
        C["ident"] = k.sb("ident", [128, 128])
        k.load(C["ident"], C["ident"][:], cid, cid[:, :])
        C["iota16"] = k.sb("iota16", [128, 16])
        k.load(C["iota16"], C["iota16"][:], cio, cio[:, :])
        C["ones1"] = k.sb("ones1", [1, 128])
        k.op("pool", lambda e: e.memset(C["ones1"][:], 1.0), wr=[C["ones1"]])
        C["ones"] = k.sb("ones", [128, 128])
        k.op("pool", lambda e: e.memset(C["ones"][:], 1.0), wr=[C["ones"]])
        mods_setup(k, C)
        k.phase_begin()
        cp = [k.sb("cp%d" % i, [128, D]) for i in range(2)]
        for t in range(T_CORE // 128):
            b = cp[t % 2]
            k.load(b, b[:], xin, xin[t * 128:(t + 1) * 128, :])
            k.store(y, y[t * 128:(t + 1) * 128, :], b, b[:])
        k.phase_end()
        groups = [(s_[0], s_[1], s_[2]) for s_ in SEQS]
        for l in range(depth):
            mods_phase(k, C, l)
            if mixer:
                if l % 2 == 0:
                    even_phase(k, C, l, y, SEQS)
                else:
                    mla_phase(k, C, l, y, SEQS)
            if peer:
                peer_phase(k, C, l, y, T_CORE, groups)
        k.barrier()
        k.emit()
    return nc


def core_inputs(inp, c, consts):
    m = dict(consts)
    for n in WNAMES:
        m[n] = inp[n]
    m["x_all"] = np.ascontiguousarray(np.concatenate(
        [inp["x_sample"][c], inp["x_prompt"][2 * c], inp["x_prompt"][2 * c + 1]], 0))
    m["cvec"] = np.ascontiguousarray(np.stack([inp["c"][c], inp["c_ctx"]], 0))
    m["state_gdn"] = np.ascontiguousarray(inp["state_gdn"][c])
    m["cache_ckv"] = np.ascontiguousarray(inp["cache_mla_ckv"][c])
    m["cache_kr"] = np.ascontiguousarray(inp["cache_mla_krope"][c])
    return m


def all_consts():
    consts = {"c_ident": np.eye(128, dtype=np.float32),
              "c_iota16": np.tile(np.arange(16, dtype=np.float32), (128, 1)), "c_rope": rope_table()}
    consts.update(even_consts())
    return consts


def kernel(**inputs):
    inp = {n: np.ascontiguousarray(np.asarray(v)) for n, v in inputs.items()}
    nc = build_program()
    consts = {"c_ident": np.eye(128, dtype=np.float32),
              "c_iota16": np.tile(np.arange(16, dtype=np.float32), (128, 1)), "c_rope": rope_table()}
    consts.update(even_consts())
    in_maps = []
    for c in range(8):
        m = dict(consts)
        for n in WNAMES:
            m[n] = inp[n]
        m["x_all"] = np.ascontiguousarray(np.concatenate(
            [inp["x_sample"][c], inp["x_prompt"][2 * c], inp["x_prompt"][2 * c + 1]], 0))
        m["cvec"] = np.ascontiguousarray(np.stack([inp["c"][c], inp["c_ctx"]], 0))
        m["state_gdn"] = np.ascontiguousarray(inp["state_gdn"][c])
        m["cache_ckv"] = np.ascontiguousarray(inp["cache_mla_ckv"][c])
        m["cache_kr"] = np.ascontiguousarray(inp["cache_mla_krope"][c])
        in_maps.append(m)
    res = run_bass_kernel_spmd(nc, in_maps, core_ids=list(range(8)))
    outs = res.results
    y_sample = np.stack([outs[c]["y_all"][:4096] for c in range(8)], 0).astype(np.float32)
    y_prompt = np.concatenate([outs[c]["y_all"][4096:].reshape(2, 256, D) for c in range(8)], 0).astype(np.float32)
    new_gdn = np.concatenate([outs[c]["out_gdn"] for c in range(8)], 0).astype(np.float32)
    new_ckv = np.concatenate([outs[c]["out_ckv"] for c in range(8)], 0).astype(np.float32)
    new_kr = np.concatenate([outs[c]["out_kr"] for c in range(8)], 0).astype(np.float32)
    return (y_prompt, y_sample, new_gdn, new_ckv, new_kr)
```
